# Optimizing a Trainium2 kernel written in Bass

```python
import jax
import jax.numpy as jnp
from jax import lax
import numpy as np


D_MODEL = 2048
BATCH = 4
SEQ = 8192
DEPTH = 4

CTX_LEN = 256
GRID_W = 64
MIX_WIDTH = D_MODEL
RET_WIDTH = MIX_WIDTH // 2
CONV_WIDTH = MIX_WIDTH - RET_WIDTH
RET_HEAD_DIM = 128
RET_HEADS = RET_WIDTH // RET_HEAD_DIM
RET_CHUNK = 128
CONV_K = 31
FFN_K = 3
D_FF = ((8 * D_MODEL // 3 + 255) // 256) * 256
IN_WIDTH = 4 * RET_WIDTH + 2 * CONV_WIDTH
N_MOD = 6
ROPE_THETA = 10000.0
EPS = 1e-6

kernel_name = "hybrid_retention_conformer_dit"


def rmsnorm(x, g):
    xf = x.astype(jnp.float32)
    xf = xf * lax.rsqrt(jnp.mean(xf * xf, axis=-1, keepdims=True) + EPS)
    return (xf * g.astype(jnp.float32)).astype(x.dtype)


def layernorm(x, g, b):
    xf = x.astype(jnp.float32)
    mu = jnp.mean(xf, axis=-1, keepdims=True)
    var = jnp.mean(jnp.square(xf - mu), axis=-1, keepdims=True)
    y = (xf - mu) * lax.rsqrt(var + EPS) * g.astype(jnp.float32) + b.astype(jnp.float32)
    return y.astype(x.dtype)


def modulate(h, shift, scale):
    return h * (1 + scale) + shift


def rope_tables(pos, dim):
    inv_freq = 1.0 / (ROPE_THETA ** (jnp.arange(0, dim // 2, dtype=jnp.float32) / (dim // 2)))
    ang = pos.astype(jnp.float32)[:, None] * inv_freq[None, :]
    return jnp.cos(ang)[:, None, :], jnp.sin(ang)[:, None, :]


def apply_rope(x, cos, sin):
    half = x.shape[-1] // 2
    x1, x2 = x[..., :half], x[..., half:]
    cos = cos.astype(x.dtype)
    sin = sin.astype(x.dtype)
    return jnp.concatenate([x1 * cos - x2 * sin, x1 * sin + x2 * cos], axis=-1)


def rope_2d(x, row_cs, col_cs):
    half = x.shape[-1] // 2
    return jnp.concatenate([apply_rope(x[..., :half], *row_cs), apply_rope(x[..., half:], *col_cs)], axis=-1)


def dwconv1d(x, w, b):
    y = lax.conv_general_dilated(x, w[:, None, :].astype(x.dtype), window_strides=(1,), padding='SAME',
                                 dimension_numbers=('NWC', 'WIO', 'NWC'), feature_group_count=x.shape[-1])
    return y + b.astype(x.dtype)


def dwconv2d(x, w, b):
    y = lax.conv_general_dilated(x, w[:, :, None, :].astype(x.dtype), window_strides=(1, 1), padding='SAME',
                                 dimension_numbers=('NHWC', 'HWIO', 'NHWC'), feature_group_count=x.shape[-1])
    return y + b.astype(x.dtype)


def retention_chunkwise(q, k, v, log_gamma, s0):
    bsz, nh, length, dh = q.shape
    n_chunks = length // RET_CHUNK
    q = q.astype(jnp.float32)
    k = k.astype(jnp.float32) * (dh ** -0.5)
    v = v.astype(jnp.float32)
    to_chunks = lambda t: jnp.moveaxis(t.reshape(bsz, nh, n_chunks, RET_CHUNK, dh), 2, 0)
    idx = jnp.arange(RET_CHUNK, dtype=jnp.float32)
    diff = idx[:, None] - idx[None, :]
    lg = log_gamma[:, None, None]
    inner_decay = jnp.where(diff >= 0, jnp.exp(lg * jnp.maximum(diff, 0.0)), 0.0)
    q_decay = jnp.exp(log_gamma[:, None] * (idx + 1.0))[:, :, None]
    k_decay = jnp.exp(log_gamma[:, None] * (RET_CHUNK - 1.0 - idx))[:, :, None]
    chunk_decay = jnp.exp(log_gamma * RET_CHUNK)[:, None, None]

    def step(s, qkv):
        qc, kc, vc = qkv
        scores = jnp.einsum('bhnd,bhmd->bhnm', qc, kc) * inner_decay
        o = jnp.einsum('bhnm,bhmd->bhnd', scores, vc) + jnp.einsum('bhnd,bhde->bhne', qc, s) * q_decay
        s = s * chunk_decay + jnp.einsum('bhmd,bhme->bhde', kc * k_decay, vc)
        return s, o

    s_final, o = lax.scan(step, s0.astype(jnp.float32), (to_chunks(q), to_chunks(k), to_chunks(v)))
    o = jnp.moveaxis(o, 0, 2).reshape(bsz, nh, length, dh)
    return o, s_final


def bidir_retention(q, k, v, lg_f, lg_b, s0_f, s0_b):
    o_f, s_f = retention_chunkwise(q, k, v, lg_f, s0_f)
    rev = lambda t: jnp.flip(t, axis=2)
    o_b, s_b = retention_chunkwise(rev(q), rev(k), rev(v), lg_b, s0_b)
    return o_f + rev(o_b), s_f, s_b


def retention_output(o, g):
    o = o * lax.rsqrt(jnp.mean(o * o, axis=-1, keepdims=True) + EPS)
    bsz, nh, length, dh = o.shape
    o = jnp.transpose(o, (0, 2, 1, 3)).reshape(bsz, length, nh * dh)
    return (jax.nn.silu(g.astype(jnp.float32)) * o).astype(g.dtype)


def to_heads(t):
    bsz, length, _ = t.shape
    return t.reshape(bsz, length, RET_HEADS, RET_HEAD_DIM)


def heads_first(t):
    return jnp.transpose(t, (0, 2, 1, 3))


def conformer_conv(a, b, w, bias, ln_g, ln_b):
    u = a * jax.nn.sigmoid(b)
    u = dwconv1d(u, w, bias)
    return jax.nn.silu(layernorm(u, ln_g, ln_b))


def conv_ffn(h, w_up, w_down, conv):
    gate, val = jnp.split(h @ w_up, 2, axis=-1)
    return (jax.nn.silu(conv(gate)) * val) @ w_down


def split_mixer_proj(p):
    return jnp.split(p, [RET_WIDTH, 2 * RET_WIDTH, 3 * RET_WIDTH, 4 * RET_WIDTH, 4 * RET_WIDTH + CONV_WIDTH], axis=-1)


def setup_inputs(seed: int = 0) -> dict:
    key = jax.random.key(seed)
    ks = jax.random.split(key, 24)
    f32 = jnp.float32
    nrm = lambda k, shape, s: s * jax.random.normal(k, shape, f32)
    decay_base = jnp.log(jnp.exp2(5.0 + jnp.arange(RET_HEADS, dtype=f32)) - 1.0)
    return {
        'x': nrm(ks[0], (BATCH, SEQ, D_MODEL), 1.0),
        'c': nrm(ks[1], (BATCH, D_MODEL), 1.0),
        'ctx': nrm(ks[2], (BATCH, CTX_LEN, D_MODEL), 1.0),
        'c_ctx': nrm(ks[3], (D_MODEL,), 1.0),
        'w_mod': nrm(ks[4], (DEPTH, D_MODEL, N_MOD * D_MODEL), 0.5 * D_MODEL ** -0.5),
        'b_mod': nrm(ks[5], (DEPTH, N_MOD * D_MODEL), 0.02),
        'norm1_g': 1.0 + nrm(ks[6], (DEPTH, D_MODEL), 0.02),
        'norm2_g': 1.0 + nrm(ks[7], (DEPTH, D_MODEL), 0.02),
        'w_in': nrm(ks[8], (DEPTH, D_MODEL, IN_WIDTH), D_MODEL ** -0.5),
        'ret_decay_f': decay_base + nrm(ks[9], (DEPTH, RET_HEADS), 0.1),
        'ret_decay_b': decay_base + nrm(ks[10], (DEPTH, RET_HEADS), 0.1),
        'conv_dw_w': nrm(ks[11], (DEPTH, CONV_K, CONV_WIDTH), CONV_K ** -0.5),
        'conv_dw_b': nrm(ks[12], (DEPTH, CONV_WIDTH), 0.02),
        'conv_ln_g': 1.0 + nrm(ks[13], (DEPTH, CONV_WIDTH), 0.02),
        'conv_ln_b': nrm(ks[14], (DEPTH, CONV_WIDTH), 0.02),
        'w_out': nrm(ks[15], (DEPTH, MIX_WIDTH, D_MODEL), MIX_WIDTH ** -0.5),
        'ffn_w_up': nrm(ks[16], (DEPTH, D_MODEL, 2 * D_FF), D_MODEL ** -0.5),
        'ffn_dw_w': nrm(ks[17], (DEPTH, FFN_K, FFN_K, D_FF), 1.0 / FFN_K),
        'ffn_dw_b': nrm(ks[18], (DEPTH, D_FF), 0.02),
        'ffn_w_down': nrm(ks[19], (DEPTH, D_FF, D_MODEL), D_FF ** -0.5),
        'final_norm_g': 1.0 + nrm(ks[20], (D_MODEL,), 0.02),
    }


def reference(x, c, ctx, c_ctx, w_mod, b_mod, norm1_g, norm2_g, w_in, ret_decay_f, ret_decay_b,
              conv_dw_w, conv_dw_b, conv_ln_g, conv_ln_b, w_out, ffn_w_up, ffn_dw_w, ffn_dw_b,
              ffn_w_down, final_norm_g):
    bsz, length, _ = x.shape
    rows = length // GRID_W
    t = jnp.arange(length)
    row_cs = rope_tables(t // GRID_W, RET_HEAD_DIM // 2)
    col_cs = rope_tables(t % GRID_W, RET_HEAD_DIM // 2)
    silu_c = jax.nn.silu(c)
    silu_cc = jax.nn.silu(c_ctx)[None, :]
    zero_state = jnp.zeros((bsz, RET_HEADS, RET_HEAD_DIM, RET_HEAD_DIM), jnp.float32)
    xc = ctx
    for l in range(DEPTH):
        last = l == DEPTH - 1
        mod = (silu_c @ w_mod[l] + b_mod[l])[:, None, :]
        mod_c = (silu_cc @ w_mod[l] + b_mod[l])[:, None, :]
        sh1, sc1, g1, sh2, sc2, g2 = jnp.split(mod, N_MOD, axis=-1)
        csh1, csc1, cg1, csh2, csc2, cg2 = jnp.split(mod_c, N_MOD, axis=-1)

        hx = modulate(rmsnorm(x, norm1_g[l]), sh1, sc1)
        hc = modulate(rmsnorm(xc, norm1_g[l]), csh1, csc1)
        q, k, v, g, a, bg = split_mixer_proj(hx @ w_in[l])
        cq, ck, cv, cgt, ca, cb = split_mixer_proj(hc @ w_in[l])
        lg_f = jax.nn.log_sigmoid(ret_decay_f[l].astype(jnp.float32))
        lg_b = jax.nn.log_sigmoid(ret_decay_b[l].astype(jnp.float32))

        co, s_f, s_b = bidir_retention(heads_first(to_heads(cq)), heads_first(to_heads(ck)),
                                       heads_first(to_heads(cv)), lg_f, lg_b, zero_state, zero_state)
        q = heads_first(rope_2d(to_heads(q), row_cs, col_cs))
        k = heads_first(rope_2d(to_heads(k), row_cs, col_cs))
        o, _, _ = bidir_retention(q, k, heads_first(to_heads(v)), lg_f, lg_b, s_f, s_b)

        mix = jnp.concatenate([
            retention_output(o, g),
            conformer_conv(a, bg, conv_dw_w[l], conv_dw_b[l], conv_ln_g[l], conv_ln_b[l]),
        ], axis=-1)
        x = x + g1 * (mix @ w_out[l])

        lat_conv = lambda u: dwconv2d(u.reshape(bsz, rows, GRID_W, D_FF), ffn_dw_w[l], ffn_dw_b[l]).reshape(bsz, length, D_FF)
        x = x + g2 * conv_ffn(modulate(rmsnorm(x, norm2_g[l]), sh2, sc2), ffn_w_up[l], ffn_w_down[l], lat_conv)

        if not last:
            cmix = jnp.concatenate([
                retention_output(co, cgt),
                conformer_conv(ca, cb, conv_dw_w[l], conv_dw_b[l], conv_ln_g[l], conv_ln_b[l]),
            ], axis=-1)
            xc = xc + cg1 * (cmix @ w_out[l])
            ctx_conv = lambda u: dwconv1d(u, ffn_dw_w[l][FFN_K // 2], ffn_dw_b[l])
            xc = xc + cg2 * conv_ffn(modulate(rmsnorm(xc, norm2_g[l]), csh2, csc2), ffn_w_up[l], ffn_w_down[l], ctx_conv)

    return rmsnorm(x, final_norm_g)
```

```python
import numpy as np
from contextlib import ExitStack
import concourse.bass as bass
import concourse.mybir as mybir
from concourse.bass_utils import run_bass_kernel_spmd

F32 = mybir.dt.float32
BF16 = mybir.dt.bfloat16
AF = mybir.ActivationFunctionType
ALU = mybir.AluOpType

D = 2048
KC = 16
NH = 8
DH = 128
RW = 1024
CW = 1024
DFF = 5632
FC = 44
CTX = 256
GW = 64
CK = 31
EPS = 1e-6
THETA = 10000.0
NMOD = 6


class Sem:
    def __init__(self, h, name):
        self.h = h
        self.name = name
        self.count = 0


class Eng:
    def __init__(self, name, e, sem):
        self.name = name
        self.e = e
        self.sem = sem
        self.waited = {}


class Buf:
    def __init__(self, t, dsem=None):
        self.t = t
        self.w = None
        self.r = {}
        self.dsem = dsem

    def __getitem__(self, idx):
        return self.t[idx]


class KB:
    def __init__(self, nc):
        self.nc = nc
        self.stack = ExitStack()
        self.uid = 0
        self.sems = {}
        self.pe = self._eng("pe", nc.tensor)
        self.act = self._eng("act", nc.scalar)
        self.dve = self._eng("dve", nc.vector)
        self.pool = self._eng("pool", nc.gpsimd)
        self.sp = self._eng("sp", nc.sync)
        self.engs = [self.pe, self.act, self.dve, self.pool, self.sp]
        self.nobarrier = set()
        self.dpool_i = 0
        self.dpool_base = 0

    def new_phase(self):
        self.dpool_i = self.dpool_base

    def sem(self, name):
        if name not in self.sems:
            h = self.stack.enter_context(self.nc.semaphore(name))
            self.sems[name] = Sem(h, name)
        return self.sems[name]

    def _eng(self, name, e):
        return Eng(name, e, self.sem("eng_" + name))

    def sb(self, st, name, shape, dtype, dma=False):
        self.uid += 1
        t = st.enter_context(self.nc.sbuf_tensor(f"{name}_{self.uid}", list(shape), dtype))
        dsem = None
        if dma:
            dsem = self.sem(f"dslot{self.dpool_i}")
            self.dpool_i += 1
        return Buf(t, dsem)

    def ps(self, st, name, shape, dtype):
        self.uid += 1
        t = st.enter_context(self.nc.psum_tensor(f"{name}_{self.uid}", list(shape), dtype))
        return Buf(t)

    def _wait(self, E, deps):
        need = {}
        for d in deps:
            if d is None:
                continue
            S, v = d
            if need.get(S, 0) < v:
                need[S] = v
        for S, v in need.items():
            if E.waited.get(S, 0) < v:
                E.e.wait_ge(S.h, v)
                E.waited[S] = v

    def _deps(self, reads, writes):
        deps = []
        for b in reads:
            deps.append(b.w)
        for b in writes:
            deps.append(b.w)
            deps.extend(b.r.items())
        return deps

    def _commit(self, tok, reads, writes):
        S, v = tok
        for b in writes:
            b.w = tok
            b.r = {}
        for b in reads:
            if b.r.get(S, 0) < v:
                b.r[S] = v

    def op(self, E, fn, reads=(), writes=()):
        self._wait(E, self._deps(reads, writes))
        ins = fn(E.e)
        E.sem.count += 1
        ins.then_inc(E.sem.h, 1)
        self._commit((E.sem, E.sem.count), reads, writes)

    def mm(self, outs, reads, writes):
        E = self.pe
        self._wait(E, self._deps(reads, writes))
        ins = None
        for (o, l, r, s0, s1) in outs:
            ins = E.e.matmul(o, l, r, start=s0, stop=s1)
        E.sem.count += 1
        ins.then_inc(E.sem.h, 1)
        self._commit((E.sem, E.sem.count), reads, writes)

    def tr(self, outs, ident, reads, writes):
        E = self.pe
        self._wait(E, self._deps(list(reads) + [ident], writes))
        ins = None
        for (o, i) in outs:
            ins = E.e.transpose(o, i, ident.t[:])
        E.sem.count += 1
        ins.then_inc(E.sem.h, 1)
        self._commit((E.sem, E.sem.count), list(reads) + [ident], writes)

    def dma(self, Q, out, in_, reads=(), writes=(), sem=None):
        if sem is None:
            for b in list(writes) + list(reads):
                if b.dsem is not None:
                    sem = b.dsem
                    break
        assert sem is not None
        self._wait(Q, self._deps(reads, writes))
        Q.e.dma_start(out=out, in_=in_).then_inc(sem.h, 16)
        sem.count += 16
        self._commit((sem, sem.count), reads, writes)

    def barrier(self):
        for E in self.engs:
            for S in self.sems.values():
                if S.name in self.nobarrier:
                    continue
                if S.count > 0 and E.waited.get(S, 0) < S.count:
                    E.e.wait_ge(S.h, S.count)
                    E.waited[S] = S.count


def _rope_tables(L):
    t = np.arange(L)
    inv = 1.0 / (THETA ** (np.arange(32, dtype=np.float32) / 32.0)).astype(np.float32)
    inv = inv.astype(np.float32)
    ang_r = (t // GW).astype(np.float32)[:, None] * inv[None, :]
    ang_c = (t % GW).astype(np.float32)[:, None] * inv[None, :]
    cos = np.zeros((128, L), np.float32)
    sins = np.zeros((128, L), np.float32)
    cr, sr = np.cos(ang_r).T, np.sin(ang_r).T
    cc, sc = np.cos(ang_c).T, np.sin(ang_c).T
    cos[0:32], cos[32:64], cos[64:96], cos[96:128] = cr, cr, cc, cc
    sins[0:32], sins[32:64], sins[64:96], sins[96:128] = -sr, sr, -sc, sc
    return cos, sins


def _consts():
    c = {}
    idx = np.arange(128, dtype=np.float32)
    diff = idx[None, :] - idx[:, None]
    c["c_dpos"] = np.maximum(diff, 0.0).astype(np.float32)
    c["c_dneg"] = np.maximum(-diff, 0.0).astype(np.float32)
    c["c_ge"] = (diff >= 0).astype(np.float32)
    c["c_le"] = (diff <= 0).astype(np.float32)
    c["c_n1"] = np.broadcast_to(idx[None, :] + 1.0, (128, 128)).astype(np.float32).copy()
    c["c_cn"] = np.broadcast_to(128.0 - idx[None, :], (128, 128)).astype(np.float32).copy()
    c["c_kf"] = np.broadcast_to((127.0 - idx)[:, None], (128, 128)).astype(np.float32).copy()
    c["c_kb"] = np.broadcast_to(idx[:, None], (128, 128)).astype(np.float32).copy()
    perm = np.zeros((128, 128), np.float32)
    for p in range(128):
        q = p + 32 if (p % 64) < 32 else p - 32
        perm[q, p] = 1.0
    c["c_perm"] = perm
    c["c_ident"] = np.eye(128, dtype=np.float32)
    return c


CONST_NAMES = ["c_dpos", "c_dneg", "c_ge", "c_le", "c_n1", "c_cn", "c_kf", "c_kb", "c_perm", "c_ident"]


class Seq:
    pass


def build_program(L, depth, ncores=8, TT=512):
    nc = bass.Bass("TRN2", target_bir_lowering=False)
    K = KB(nc)

    def din(name, shape, dt=F32):
        return nc.dram_tensor(name, list(shape), dt, kind="ExternalInput").ap()

    def dint(name, shape, dt):
        return nc.dram_tensor(name, list(shape), dt, kind="Internal").ap()

    xT_in = din("xT", [D, L])
    cxT_in = din("ctxT", [D, CTX])
    cvec = din("cvec", [128, KC, 2])
    wmod = din("w_mod", [depth, 24, 128, KC, 512])
    bmod = din("b_mod", [depth, 128, 96])
    n1g = din("norm1_g", [depth, 128, KC])
    n2g = din("norm2_g", [depth, 128, KC])
    w_in = din("w_in", [depth, 12, 128, KC * 512])
    w_out = din("w_out", [depth, 4, 128, KC * 512])
    w_up = din("w_up", [depth, 22, 128, KC * 512])
    w_dn = din("w_down", [depth, 16, 128, FC * 128])
    decf = din("ret_decay_f", [depth, NH])
    decb = din("ret_decay_b", [depth, NH])
    cdw = din("conv_dw_w", [depth, 128, 8, CK])
    cdb = din("conv_dw_b", [depth, 128, 8])
    clg = din("conv_ln_g", [depth, 128, 8])
    clb = din("conv_ln_b", [depth, 128, 8])
    fdw = din("ffn_dw_w", [depth, 128, FC, 9])
    fdb = din("ffn_dw_b", [depth, 128, FC])
    fng = din("final_norm_g", [128, KC])
    flags_in = din("flags", [128, 2])
    ropec = din("rope_c", [128, L])
    ropes = din("rope_s", [128, L])
    cst_in = {n: din(n, [128, 128]) for n in CONST_NAMES}
    outT = nc.dram_tensor("outT", [D, L], F32, kind="ExternalOutput").ap()

    wb_in = dint("wb_in", [2, 12, 128, KC * 512], BF16)
    wb_out = dint("wb_out", [2, 4, 128, KC * 512], BF16)
    wb_up = dint("wb_up", [2, 22, 128, KC * 512], BF16)
    wb_dn = dint("wb_dn", [2, 16, 128, FC * 128], BF16)

    stx = dint("stx", [256, RW], F32)
    stg = dint("stg", [512, RW], F32)
    uhx = dint("uhx", [2 * CW, 16], BF16)
    uhg = dint("uhg", [4 * CW, 16], BF16)
    ghx = dint("ghx", [2 * DFF, GW], BF16)
    ghg = dint("ghg", [4 * DFF, GW], BF16)
    seqs = []
    for nm, Ls, src, rope in (("c", CTX, cxT_in, False), ("x", L, xT_in, True)):
        s = Seq()
        s.name = nm
        s.L = Ls
        s.src = src
        s.rope = rope
        s.is_ctx = not rope
        s.TT = min(TT, Ls)
        s.nt = Ls // s.TT
        s.cpt = s.TT // 128
        s.nch = Ls // 128
        s.xs = dint(f"xs_{nm}", [D, Ls], F32)
        s.qT = dint(f"qT_{nm}", [RW, Ls], BF16)
        s.kT = dint(f"kT_{nm}", [RW, Ls], BF16)
        s.qfT = dint(f"qfT_{nm}", [RW, Ls], BF16)
        s.qbT = dint(f"qbT_{nm}", [RW, Ls], BF16)
        s.ktok = dint(f"ktok_{nm}", [Ls, RW], BF16)
        s.vtok = dint(f"vtok_{nm}", [Ls, RW], BF16)
        s.sgT = dint(f"sgT_{nm}", [RW, Ls], BF16)
        s.uT = dint(f"uT_{nm}", [CW, Ls], BF16)
        s.SbD = dint(f"SbD_{nm}", [Ls // 128, 128, RW], BF16)
        s.SfD = dint(f"SfD_{nm}", [Ls // 128, 128, RW], BF16)
        s.mixT = dint(f"mixT_{nm}", [D, Ls], BF16)
        s.gateT = dint(f"gateT_{nm}", [DFF, Ls], BF16)
        s.valT = dint(f"valT_{nm}", [DFF, Ls], BF16)
        s.actT = dint(f"actT_{nm}", [DFF, Ls], BF16)
        s.v = 1 if s.is_ctx else 0
        seqs.append(s)
    sctx, slat = seqs

    top = K.stack
    SP, ACT, DVE, POOL, PE = K.sp, K.act, K.dve, K.pool, K.pe

    cst = {n: K.sb(top, n, [128, 128], F32, dma=True) for n in CONST_NAMES}
    modT = K.sb(top, "modT", [128, depth, 96, 2], F32)
    gs1 = K.sb(top, "gs1", [128, depth, KC, 2], F32)
    gs2 = K.sb(top, "gs2", [128, depth, KC, 2], F32)
    ones_bf = K.sb(top, "ones_bf", [128, 128], BF16)
    ones_f = K.sb(top, "ones_f", [128, 128], F32)
    ident_bf = K.sb(top, "ident_bf", [128, 128], BF16)
    epsb = K.sb(top, "epsb", [128, 1], F32)
    sfin = K.sb(top, "sfin", [128, RW], F32)
    sbin = K.sb(top, "sbin", [128, RW], F32)
    n1g_sb = K.sb(top, "n1g", [128, depth, KC], F32, dma=True)
    n2g_sb = K.sb(top, "n2g", [128, depth, KC], F32, dma=True)
    bmod_sb = K.sb(top, "bmod", [128, depth, 96], F32, dma=True)
    cvec_sb = K.sb(top, "cvec", [128, KC, 2], F32, dma=True)
    fng_sb = K.sb(top, "fng", [128, KC], F32, dma=True)
    cdw_sb = K.sb(top, "cdw", [128, 8, CK], F32, dma=True)
    cdb_sb = K.sb(top, "cdb", [128, 8], F32, dma=True)
    clg_sb = K.sb(top, "clg", [128, 8], F32, dma=True)
    clb_sb = K.sb(top, "clb", [128, 8], F32, dma=True)
    fdw_sb = K.sb(top, "fdw", [128, FC, 9], F32, dma=True)
    fdb_sb = K.sb(top, "fdb", [128, FC], F32, dma=True)
    dec_sb = K.sb(top, "dec", [128, 2, NH], F32, dma=True)
    lg_sb = K.sb(top, "lg", [128, 2, NH], F32)
    MASK = K.sb(top, "MASK", [128, NH, 128], F32)
    QDF = K.sb(top, "QDF", [128, NH, 128], F32)
    QDB = K.sb(top, "QDB", [128, NH, 128], F32)
    KDF = K.sb(top, "KDF", [128, NH], F32)
    KDB = K.sb(top, "KDB", [128, NH], F32)
    CDF = K.sb(top, "CDF", [128, NH], F32)
    CDB = K.sb(top, "CDB", [128, NH], F32)

    flags_sb = K.sb(top, "flags", [128, 2], F32, dma=True)
    pairs = [[2 * i, 2 * i + 1] for i in range(ncores // 2)]
    ccs = K.sem("ccs")

    def collective(src, dst):
        K.barrier()
        POOL.e.collective_compute("AllGather", ALU.bypass, replica_groups=pairs, ins=[src], outs=[dst]).then_inc(ccs.h)
        ccs.count += 1
        K.barrier()

    K.dpool_base = K.dpool_i
    cvt_sems = {n: K.sem("cvt_" + n) for n in ("in", "out", "up", "dn")}
    for n in cvt_sems:
        K.nobarrier.add("cvt_" + n)
    cvt_need = {}
    cvt_queue = []

    def convert_weights(l):
        par = l % 2
        cnt = {n: cvt_sems[n].count + sum(16 for q in cvt_queue if q[0] == n) for n in cvt_sems}
        for (n, src, dst, nb) in (("in", w_in, wb_in, 12), ("out", w_out, wb_out, 4), ("up", w_up, wb_up, 22), ("dn", w_dn, wb_dn, 16)):
            for b in range(nb):
                cvt_queue.append((n, dst[par, b], src[l, b]))
            cnt[n] += 16 * nb
            cvt_need[(l, n)] = cnt[n]

    def cvt_pump(k):
        for _ in range(k):
            if not cvt_queue:
                return
            n, dst, src = cvt_queue.pop(0)
            sm = cvt_sems[n]
            POOL.e.dma_start(out=dst, in_=src).then_inc(sm.h, 16)
            sm.count += 16

    def cvt_wait(l, n):
        sm = cvt_sems[n]
        while sm.count < cvt_need[(l, n)]:
            cvt_pump(1)
        for E in K.engs:
            E.e.wait_ge(sm.h, cvt_need[(l, n)])

    def mod_phase():
        st = ExitStack()
        K.new_phase()
        for n in CONST_NAMES:
            K.dma(SP, cst[n].t[:], cst_in[n], writes=[cst[n]])
        K.dma(SP, n1g_sb.t[:], n1g.rearrange("l p k -> p l k"), writes=[n1g_sb])
        K.dma(SP, n2g_sb.t[:], n2g.rearrange("l p k -> p l k"), writes=[n2g_sb])
        K.dma(SP, bmod_sb.t[:], bmod.rearrange("l p k -> p l k"), writes=[bmod_sb])
        K.dma(SP, cvec_sb.t[:], cvec, writes=[cvec_sb])
        K.dma(SP, fng_sb.t[:], fng, writes=[fng_sb])
        K.dma(SP, flags_sb.t[:], flags_in, writes=[flags_sb])
        K.op(DVE, lambda e: e.memset(ones_bf.t[:], 1.0), writes=[ones_bf])
        K.op(DVE, lambda e: e.memset(ones_f.t[:], 1.0), writes=[ones_f])
        K.op(DVE, lambda e: e.memset(epsb.t[:], EPS), writes=[epsb])
        K.op(DVE, lambda e: e.tensor_copy(ident_bf.t[:], cst["c_ident"].t[:]), reads=[cst["c_ident"]], writes=[ident_bf])
        sc = K.sb(st, "silu_c", [128, KC, 2], F32)
        K.op(ACT, lambda e: e.activation(sc.t[:], cvec_sb.t[:], AF.Silu), reads=[cvec_sb], writes=[sc])
        wsl = [K.sb(st, f"wmod{i}", [128, KC, 512], F32, dma=True) for i in range(3)]
        pm = [K.ps(st, f"pmod{i}", [128, 8], F32) for i in range(2)]
        blocks = [(l, b) for l in range(depth) for b in range(24)]
        PF = 2

        def load(i):
            l, b = blocks[i]
            w = wsl[i % 3]
            K.dma(SP, w.t[:], wmod[l, b], writes=[w])

        for i in range(min(PF, len(blocks))):
            load(i)
        for i, (l, b) in enumerate(blocks):
            if i + PF < len(blocks):
                load(i + PF)
            w = wsl[i % 3]
            p = pm[i % 2]
            for j in range(4):
                K.mm([(p.t[:, 2 * j:2 * j + 2], w.t[:, kc, j * 128:(j + 1) * 128], sc.t[:, kc, :], kc == 0, kc == KC - 1)
                      for kc in range(KC)], reads=[w, sc], writes=[p])
            for j in range(4):
                jj = b * 4 + j
                K.op(DVE, lambda e, j=j, jj=jj, p=p, l=l: e.tensor_scalar(
                    modT.t[:, l, jj, :], p.t[:, 2 * j:2 * j + 2], bmod_sb.t[:, l, jj:jj + 1], None, ALU.add),
                    reads=[p, bmod_sb], writes=[modT])
        for l in range(depth):
            for v in range(2):
                K.op(DVE, lambda e, l=l, v=v: e.scalar_tensor_tensor(
                    gs1.t[:, l, :, v], modT.t[:, l, 16:32, v], 1.0, n1g_sb.t[:, l, :], ALU.add, ALU.mult),
                    reads=[modT, n1g_sb], writes=[gs1])
                K.op(DVE, lambda e, l=l, v=v: e.scalar_tensor_tensor(
                    gs2.t[:, l, :, v], modT.t[:, l, 64:80, v], 1.0, n2g_sb.t[:, l, :], ALU.add, ALU.mult),
                    reads=[modT, n2g_sb], writes=[gs2])
        K.barrier()
        st.close()

    def layer_params(l):
        K.dma(SP, cdw_sb.t[:], cdw[l], writes=[cdw_sb])
        K.dma(SP, cdb_sb.t[:], cdb[l], writes=[cdb_sb])
        K.dma(SP, clg_sb.t[:], clg[l], writes=[clg_sb])
        K.dma(SP, clb_sb.t[:], clb[l], writes=[clb_sb])
        K.dma(SP, fdw_sb.t[:], fdw[l], writes=[fdw_sb])
        K.dma(SP, fdb_sb.t[:], fdb[l], writes=[fdb_sb])
        K.dma(SP, dec_sb.t[:, 0, :], decf[l].partition_broadcast(128), writes=[dec_sb])
        K.dma(SP, dec_sb.t[:, 1, :], decb[l].partition_broadcast(128), writes=[dec_sb])
        K.op(ACT, lambda e: e.activation(lg_sb.t[:], dec_sb.t[:], AF.Exp, scale=-1.0), reads=[dec_sb], writes=[lg_sb])
        K.op(DVE, lambda e: e.tensor_scalar_add(lg_sb.t[:], lg_sb.t[:], 1.0), reads=[lg_sb], writes=[lg_sb])
        K.op(ACT, lambda e: e.activation(lg_sb.t[:], lg_sb.t[:], AF.Ln), reads=[lg_sb], writes=[lg_sb])
        K.op(DVE, lambda e: e.tensor_scalar_mul(lg_sb.t[:], lg_sb.t[:], -1.0), reads=[lg_sb], writes=[lg_sb])
        isq = float(DH ** -0.5)
        st = ExitStack()
        K.new_phase()
        tmpA = K.sb(st, "tmpA", [128, 128], F32)
        tk = K.sb(st, "tk", [128, 2, NH], F32)
        for h in range(NH):
            lf = lg_sb.t[:, 0, h:h + 1]
            lb = lg_sb.t[:, 1, h:h + 1]
            K.op(ACT, lambda e, h=h, lf=lf: e.activation(MASK.t[:, h, :], cst["c_dpos"].t[:], AF.Exp, scale=lf),
                 reads=[cst["c_dpos"], lg_sb], writes=[MASK])
            K.op(DVE, lambda e, h=h: e.tensor_tensor(MASK.t[:, h, :], MASK.t[:, h, :], cst["c_ge"].t[:], ALU.mult),
                 reads=[MASK, cst["c_ge"]], writes=[MASK])
            K.op(ACT, lambda e, h=h, lb=lb: e.activation(tmpA.t[:], cst["c_dneg"].t[:], AF.Exp, scale=lb),
                 reads=[cst["c_dneg"], lg_sb], writes=[tmpA])
            K.op(DVE, lambda e, h=h: e.tensor_tensor(tmpA.t[:], tmpA.t[:], cst["c_le"].t[:], ALU.mult),
                 reads=[tmpA, cst["c_le"]], writes=[tmpA])
            K.op(DVE, lambda e, h=h: e.tensor_tensor(MASK.t[:, h, :], MASK.t[:, h, :], tmpA.t[:], ALU.add),
                 reads=[MASK, tmpA], writes=[MASK])
            K.op(DVE, lambda e, h=h: e.tensor_scalar_mul(MASK.t[:, h, :], MASK.t[:, h, :], isq),
                 reads=[MASK], writes=[MASK])
            K.op(ACT, lambda e, h=h, lf=lf: e.activation(QDF.t[:, h, :], cst["c_n1"].t[:], AF.Exp, scale=lf),
                 reads=[cst["c_n1"], lg_sb], writes=[QDF])
            K.op(ACT, lambda e, h=h, lb=lb: e.activation(QDB.t[:, h, :], cst["c_cn"].t[:], AF.Exp, scale=lb),
                 reads=[cst["c_cn"], lg_sb], writes=[QDB])
        K.op(DVE, lambda e: e.tensor_scalar_mul(tk.t[:, 0, :], lg_sb.t[:, 0, :], cst["c_kf"].t[:, 0:1]),
             reads=[lg_sb, cst["c_kf"]], writes=[tk])
        K.op(DVE, lambda e: e.tensor_scalar_mul(tk.t[:, 1, :], lg_sb.t[:, 1, :], cst["c_kb"].t[:, 0:1]),
             reads=[lg_sb, cst["c_kb"], tk], writes=[tk])
        K.op(ACT, lambda e: e.activation(KDF.t[:], tk.t[:, 0, :], AF.Exp), reads=[tk], writes=[KDF])
        K.op(ACT, lambda e: e.activation(KDB.t[:], tk.t[:, 1, :], AF.Exp), reads=[tk], writes=[KDB])
        K.op(DVE, lambda e: e.tensor_scalar_mul(KDF.t[:], KDF.t[:], isq), reads=[KDF], writes=[KDF])
        K.op(DVE, lambda e: e.tensor_scalar_mul(KDB.t[:], KDB.t[:], isq), reads=[KDB], writes=[KDB])
        K.op(ACT, lambda e: e.activation(CDF.t[:], lg_sb.t[:, 0, :], AF.Exp, scale=128.0), reads=[lg_sb], writes=[CDF])
        K.op(ACT, lambda e: e.activation(CDB.t[:], lg_sb.t[:, 1, :], AF.Exp, scale=128.0), reads=[lg_sb], writes=[CDB])
        K.barrier()
        st.close()

    def b3(T):
        return T.t[:, :].unsqueeze(2).broadcast_to([128, NH, 128])

    def v3(ap):
        return ap.rearrange("p (h n) -> p h n", h=NH)

    def kview(ap2d, t0, n):
        return ap2d.rearrange("(k p) t -> p k t", p=128)[:, :, t0:t0 + n]

    def norm_mod(st_bufs, xs, hT, TTs, gs, l, v, shift_lo):
        sq, pss, rt, rstd, tmp = st_bufs
        for g4 in range(4):
            sq_ = sq[g4 % 2]
            K.op(ACT, lambda e, g4=g4, sq_=sq_: e.activation(sq_.t[:, :, :TTs], xs.t[:, g4 * 4:(g4 + 1) * 4, :TTs], AF.Square),
                 reads=[xs], writes=[sq_])
            K.mm([(pss.t[:, :TTs], ones_bf.t[:], sq_.t[:, kk, :TTs], g4 == 0 and kk == 0, g4 == 3 and kk == 3) for kk in range(4)],
                 reads=[ones_bf, sq_], writes=[pss])
        K.op(ACT, lambda e: e.activation(rt.t[:, :TTs], pss.t[:, :TTs], AF.Sqrt, bias=epsb.t[:], scale=1.0 / D),
             reads=[pss, epsb], writes=[rt])
        K.op(DVE, lambda e: e.reciprocal(rstd.t[:, :TTs], rt.t[:, :TTs]), reads=[rt], writes=[rstd])
        for kc in range(KC):
            tb = tmp[kc % 2]
            K.op(DVE, lambda e, kc=kc, tb=tb: e.scalar_tensor_tensor(
                tb.t[:, :TTs], xs.t[:, kc, :TTs], gs.t[:, l, kc, v:v + 1], rstd.t[:, :TTs], ALU.mult, ALU.mult),
                reads=[xs, gs, rstd], writes=[tb])
            K.op(ACT, lambda e, kc=kc, tb=tb: e.activation(
                hT.t[:, kc, :TTs], tb.t[:, :TTs], AF.Identity, bias=modT.t[:, l, shift_lo + kc, v:v + 1], scale=1.0),
                reads=[tb, modT], writes=[hT])

    def phase_A(s, l):
        st = ExitStack()
        K.new_phase()
        par = l % 2
        TTs = s.TT
        xsrc = s.src if l == 0 else s.xs
        xsl = [K.sb(st, "A_x", [128, KC, TTs], F32, dma=True)]
        hTs = [K.sb(st, f"A_hT{i}", [128, KC, TTs], BF16) for i in range(1)]
        sq = [K.sb(st, f"A_sq{i}", [128, 4, TTs], BF16) for i in range(2)]
        rt = K.sb(st, "A_rt", [128, TTs], F32)
        rstd = K.sb(st, "A_rstd", [128, TTs], F32)
        tmp = [K.sb(st, f"A_tmp{i}", [128, TTs], F32) for i in range(2)]
        wsl = [K.sb(st, f"A_w{i}", [128, KC, 512], BF16, dma=True) for i in range(2)]
        qst = K.sb(st, "A_qst", [128, NH, TTs], BF16, dma=True)
        kst = K.sb(st, "A_kst", [128, NH, TTs], BF16, dma=True)
        qfst = K.sb(st, "A_qfst", [128, NH, TTs], BF16, dma=True)
        qbst = K.sb(st, "A_qbst", [128, NH, TTs], BF16, dma=True)
        sgst = K.sb(st, "A_sgst", [128, 4, TTs], BF16, dma=True)
        ust = K.sb(st, "A_ust", [128, 2, TTs], BF16, dma=True)
        ktst = K.sb(st, "A_ktst", [128, s.cpt, RW], BF16, dma=True)
        vtst = K.sb(st, "A_vtst", [128, s.cpt, RW], BF16, dma=True)
        a_sb = [K.sb(st, f"A_a{i}", [128, TTs], F32) for i in range(2)]
        sig = K.sb(st, "A_sig", [128, TTs], F32)
        qf32 = [K.sb(st, f"A_qf{i}", [128, TTs], F32) for i in range(2)]
        t1 = [K.sb(st, f"A_t1{i}", [128, TTs], F32) for i in range(1)] * 2
        t2 = [K.sb(st, f"A_t2{i}", [128, TTs], F32) for i in range(1)] * 2
        cosb = K.sb(st, "A_cos", [128, TTs], F32, dma=True)
        sinb = K.sb(st, "A_sin", [128, TTs], F32, dma=True)
        pss = K.ps(st, "A_pss", [128, 512], F32)
        pacc = [K.ps(st, f"A_pacc{i}", [128, 512], F32) for i in range(4)]
        prope = [K.ps(st, f"A_prope{i}", [128, 512], F32) for i in range(2)]
        ptr = K.ps(st, "A_ptr", [128, RW], BF16)
        acc_i = 0
        w_i = 0
        cvt_wait(l, "in")
        for ti in range(s.nt):
            t0 = ti * TTs
            xs = xsl[0]
            hT = hTs[0]
            if not s.is_ctx:
                cvt_pump(3)
            K.dma(SP, xs.t[:], kview(xsrc, t0, TTs), writes=[xs])
            if s.rope:
                K.dma(SP, cosb.t[:], ropec[:, t0:t0 + TTs], writes=[cosb])
                K.dma(SP, sinb.t[:], ropes[:, t0:t0 + TTs], writes=[sinb])
            norm_mod((sq, pss, rt, rstd, tmp), xs, hT, TTs, gs1, l, s.v, 0)
            rope_i = 0
            for b in range(12):
                w = wsl[w_i % 2]
                w_i += 1
                K.dma(SP, w.t[:].rearrange("p k n -> p (k n)"), wb_in[par, b], writes=[w])
                if b in (4, 5):
                    for j in range(s.cpt):
                        p = pacc[acc_i % 4]
                        acc_i += 1
                        K.mm([(p.t[:, :], hT.t[:, kc, j * 128:(j + 1) * 128], w.t[:, kc, :], kc == 0, kc == KC - 1)
                              for kc in range(KC)], reads=[hT, w], writes=[p])
                        half = b - 4
                        if j % 2 == 0:
                            K.op(ACT, lambda e, j=j, p=p, half=half: e.copy(vtst.t[:, j, half * 512:(half + 1) * 512], p.t[:, :]),
                                 reads=[p], writes=[vtst])
                        else:
                            K.op(DVE, lambda e, j=j, p=p, half=half: e.tensor_copy(vtst.t[:, j, half * 512:(half + 1) * 512], p.t[:, :]),
                                 reads=[p], writes=[vtst])
                    continue
                for j in range(4):
                    p = pacc[acc_i % 4]
                    acc_i += 1
                    K.mm([(p.t[:, :TTs], w.t[:, kc, j * 128:(j + 1) * 128], hT.t[:, kc, :TTs], kc == 0, kc == KC - 1)
                          for kc in range(KC)], reads=[hT, w], writes=[p])
                    if b < 4:
                        h = (b % 2) * 4 + j
                        dst = qst if b < 2 else kst
                        if not s.rope:
                            K.op(ACT, lambda e, p=p, dst=dst, h=h: e.copy(dst.t[:, h, :TTs], p.t[:, :TTs]),
                                 reads=[p], writes=[dst])
                        else:
                            r = rope_i % 2
                            rope_i += 1
                            qf = qf32[r]
                            pr = prope[r]
                            K.op(ACT, lambda e, p=p, qf=qf: e.copy(qf.t[:, :TTs], p.t[:, :TTs]), reads=[p], writes=[qf])
                            K.mm([(pr.t[:, :TTs], cst["c_perm"].t[:], qf.t[:, :TTs], True, True)],
                                 reads=[cst["c_perm"], qf], writes=[pr])
                            K.op(POOL, lambda e, qf=qf, r=r: e.tensor_tensor(t1[r].t[:, :TTs], qf.t[:, :TTs], cosb.t[:, :TTs], ALU.mult),
                                 reads=[qf, cosb], writes=[t1[r]])
                            K.op(DVE, lambda e, pr=pr, r=r: e.tensor_tensor(t2[r].t[:, :TTs], pr.t[:, :TTs], sinb.t[:, :TTs], ALU.mult),
                                 reads=[pr, sinb], writes=[t2[r]])
                            K.op(POOL, lambda e, r=r, dst=dst, h=h: e.tensor_tensor(dst.t[:, h, :TTs], t1[r].t[:, :TTs], t2[r].t[:, :TTs], ALU.add),
                                 reads=[t1[r], t2[r]], writes=[dst])
                    if b < 2:
                        cp_ = TTs // 128
                        for (dstq, TAB) in ((qfst, QDF), (qbst, QDB)):
                            K.op(DVE, lambda e, dstq=dstq, TAB=TAB, h=h: e.tensor_tensor(
                                dstq.t[:, h, :TTs].rearrange("p (c n) -> p c n", n=128),
                                qst.t[:, h, :TTs].rearrange("p (c n) -> p c n", n=128),
                                TAB.t[:, h, :].unsqueeze(1).broadcast_to([128, cp_, 128]), ALU.mult),
                                reads=[qst, TAB], writes=[dstq])
                    if b in (6, 7):
                        K.op(ACT, lambda e, p=p, j=j: e.activation(sgst.t[:, j, :TTs], p.t[:, :TTs], AF.Silu),
                             reads=[p], writes=[sgst])
                    elif b >= 8:
                        pp = b - 8
                        if j < 2:
                            K.op(DVE, lambda e, p=p, j=j: e.tensor_copy(a_sb[j].t[:, :TTs], p.t[:, :TTs]),
                                 reads=[p], writes=[a_sb[j]])
                        else:
                            c = 2 * pp + (j - 2)
                            K.op(ACT, lambda e, p=p: e.activation(sig.t[:, :TTs], p.t[:, :TTs], AF.Sigmoid),
                                 reads=[p], writes=[sig])
                            K.op(DVE, lambda e, j=j, c=c: e.tensor_tensor(ust.t[:, j - 2, :TTs], a_sb[j - 2].t[:, :TTs], sig.t[:, :TTs], ALU.mult),
                                 reads=[a_sb[j - 2], sig], writes=[ust])
                if b in (6, 7):
                    hb = (b - 6) * 4
                    K.dma(POOL, s.sgT[hb * 128:(hb + 4) * 128, t0:t0 + TTs].rearrange("(j p) t -> p j t", p=128), sgst.t[:], reads=[sgst])
                if b >= 8:
                    c0 = 2 * (b - 8)
                    K.dma(POOL, s.uT[c0 * 128:(c0 + 2) * 128, t0:t0 + TTs].rearrange("(j p) t -> p j t", p=128), ust.t[:], reads=[ust])
                if b == 3:
                    for j in range(s.cpt):
                        K.tr([(ptr.t[:, h * 128:(h + 1) * 128], kst.t[:, h, j * 128:(j + 1) * 128]) for h in range(NH)],
                             ident_bf, reads=[kst], writes=[ptr])
                        K.op(DVE, lambda e, j=j: e.tensor_copy(ktst.t[:, j, :], ptr.t[:, :]), reads=[ptr], writes=[ktst])
            K.dma(POOL, kview(s.qT, t0, TTs), qst.t[:], reads=[qst])
            K.dma(POOL, kview(s.kT, t0, TTs), kst.t[:], reads=[kst])
            K.dma(POOL, kview(s.qfT, t0, TTs), qfst.t[:], reads=[qfst])
            K.dma(POOL, kview(s.qbT, t0, TTs), qbst.t[:], reads=[qbst])
            K.dma(POOL, s.ktok[t0:t0 + TTs, :].rearrange("(j p) f -> p j f", p=128), ktst.t[:], reads=[ktst])
            K.dma(POOL, s.vtok[t0:t0 + TTs, :].rearrange("(j p) f -> p j f", p=128), vtst.t[:], reads=[vtst])
        K.barrier()
        st.close()

    def sweep(s, direction, init_fn, store, final_fn):
        st = ExitStack()
        K.new_phase()
        fwd = direction == "f"
        S = K.sb(st, "SW_S", [128, RW], F32)
        S16 = [K.sb(st, f"SW_S16_{i}", [128, RW], BF16, dma=True) for i in range(2)]
        kt = [K.sb(st, f"SW_kt{i}", [128, RW], BF16, dma=True) for i in range(3)]
        vt = [K.sb(st, f"SW_vt{i}", [128, RW], BF16, dma=True) for i in range(3)]
        kd = [K.sb(st, f"SW_kd{i}", [128, RW], BF16) for i in range(2)]
        tmpS = K.sb(st, "SW_tmp", [128, RW], F32)
        pA = [K.ps(st, f"SW_pA{i}", [128, RW], F32) for i in range(2)]
        KD = KDF if fwd else KDB
        CD = CDF if fwd else CDB
        SD = s.SfD if fwd else s.SbD
        init_fn(S, st)
        order = list(range(s.nch)) if fwd else list(range(s.nch - 1, -1, -1))

        def load(i):
            c = order[i]
            K.dma(SP, kt[i % 3].t[:], s.ktok[c * 128:(c + 1) * 128, :], writes=[kt[i % 3]])
            K.dma(SP, vt[i % 3].t[:], s.vtok[c * 128:(c + 1) * 128, :], writes=[vt[i % 3]])

        for i in range(min(2, len(order))):
            load(i)
        for i, c in enumerate(order):
            if i + 2 < len(order):
                load(i + 2)
            k_, v_ = kt[i % 3], vt[i % 3]
            kd_ = kd[i % 2]
            pA_ = pA[i % 2]
            if store:
                s16 = S16[i % 2]
                K.op(ACT, lambda e, s16=s16: e.copy(s16.t[:], S.t[:]), reads=[S], writes=[s16])
                K.dma(POOL, SD[c], s16.t[:], reads=[s16])
            K.op(DVE, lambda e, k_=k_, kd_=kd_: e.tensor_tensor(v3(kd_.t[:]), v3(k_.t[:]), b3(KD), ALU.mult),
                 reads=[k_, KD], writes=[kd_])
            K.mm([(pA_.t[:, h * 128:(h + 1) * 128], kd_.t[:, h * 128:(h + 1) * 128], v_.t[:, h * 128:(h + 1) * 128], True, True)
                  for h in range(NH)], reads=[kd_, v_], writes=[pA_])
            K.op(DVE, lambda e: e.tensor_tensor(v3(tmpS.t[:]), v3(S.t[:]), b3(CD), ALU.mult),
                 reads=[S, CD], writes=[tmpS])
            K.op(DVE, lambda e, pA_=pA_: e.tensor_tensor(S.t[:], tmpS.t[:], pA_.t[:], ALU.add), reads=[tmpS, pA_], writes=[S])
        final_fn(S, st)
        K.barrier()
        st.close()

    def phase_B1(s, l):
        if s.is_ctx:
            def zero(S, st):
                K.op(DVE, lambda e: e.memset(S.t[:], 0.0), writes=[S])
            sweep(s, "f", zero, True, lambda S, st: K.op(DVE, lambda e: e.tensor_copy(sfin.t[:], S.t[:]), reads=[S], writes=[sfin]))
            sweep(s, "b", zero, True, lambda S, st: K.op(DVE, lambda e: e.tensor_copy(sbin.t[:], S.t[:]), reads=[S], writes=[sbin]))
            return
        def mk_init1(src, fcol):
            def f(S, st):
                K.op(DVE, lambda e: e.tensor_scalar_mul(S.t[:], src.t[:], flags_sb.t[:, fcol:fcol + 1]), reads=[src, flags_sb], writes=[S])
            return f

        def mk_final1(idx):
            def f(S, st):
                K.dma(SP, stx[idx * 128:(idx + 1) * 128, :], S.t[:], reads=[S], sem=K.sem("d_xch"))
            return f

        sweep(s, "f", mk_init1(sfin, 0), False, mk_final1(0))
        sweep(s, "b", mk_init1(sbin, 1), False, mk_final1(1))
        collective(stx, stg)

        def mk_init2(src, fown, fpart, grow):
            def f(S, st):
                g = K.sb(st, "SW_g", [128, RW], F32, dma=True)
                K.dma(SP, g.t[:], stg[grow * 128:(grow + 1) * 128, :], writes=[g])
                K.op(DVE, lambda e: e.tensor_scalar_mul(S.t[:], g.t[:], flags_sb.t[:, fpart:fpart + 1]), reads=[g, flags_sb], writes=[S])
                K.op(DVE, lambda e: e.scalar_tensor_tensor(S.t[:], src.t[:], flags_sb.t[:, fown:fown + 1], S.t[:], ALU.mult, ALU.add),
                     reads=[src, flags_sb, S], writes=[S])
            return f

        nop = lambda S, st: None
        sweep(s, "f", mk_init2(sfin, 0, 1, 0), True, nop)
        sweep(s, "b", mk_init2(sbin, 1, 0, 3), True, nop)

    def phase_B2(s, l):
        st = ExitStack()
        K.new_phase()
        TTs = s.TT
        cpt = s.cpt
        qTt = [K.sb(st, f"B2_qT{i}", [128, NH, TTs], BF16, dma=True) for i in range(2)]
        kTt = [K.sb(st, f"B2_kT{i}", [128, NH, TTs], BF16, dma=True) for i in range(2)]
        qfTt = [K.sb(st, f"B2_qfT{i}", [128, NH, TTs], BF16, dma=True) for i in range(2)]
        qbTt = [K.sb(st, f"B2_qbT{i}", [128, NH, TTs], BF16, dma=True) for i in range(2)]
        sgt = [K.sb(st, f"B2_sg{i}", [128, NH, TTs], BF16, dma=True) for i in range(2)]
        vtt = [K.sb(st, f"B2_vt{i}", [128, cpt, RW], BF16, dma=True) for i in range(2)]
        sft = [K.sb(st, f"B2_sf{i}", [128, cpt, RW], BF16, dma=True) for i in range(2)]
        sbt = [K.sb(st, f"B2_sb{i}", [128, cpt, RW], BF16, dma=True) for i in range(2)]
        mixst = [K.sb(st, f"B2_mix{i}", [128, NH, TTs], BF16, dma=True) for i in range(2)]
        Pb = [K.sb(st, f"B2_P{i}", [128, NH, 128], BF16) for i in range(2)]
        sqob = [K.sb(st, f"B2_sqo{i}", [128, NH, 128], BF16) for i in range(2)]
        rt = K.sb(st, "B2_rt", [128, NH, 128], F32)
        rinv = K.sb(st, "B2_rinv", [128, NH, 128], F32)
        on = K.sb(st, "B2_on", [128, NH, 128], F32)
        pSC = K.ps(st, "B2_pSC", [128, RW], F32)
        pOb = [K.ps(st, f"B2_pO{i}", [128, RW], F32) for i in range(2)]
        pSS = K.ps(st, "B2_pSS", [128, RW], F32)

        def load(ti):
            t0 = ti * TTs
            i = ti % 2
            K.dma(SP, qTt[i].t[:], kview(s.qT, t0, TTs), writes=[qTt[i]])
            K.dma(SP, kTt[i].t[:], kview(s.kT, t0, TTs), writes=[kTt[i]])
            K.dma(SP, qfTt[i].t[:], kview(s.qfT, t0, TTs), writes=[qfTt[i]])
            K.dma(SP, qbTt[i].t[:], kview(s.qbT, t0, TTs), writes=[qbTt[i]])
            K.dma(SP, sgt[i].t[:], kview(s.sgT, t0, TTs), writes=[sgt[i]])
            K.dma(SP, vtt[i].t[:], s.vtok[t0:t0 + TTs, :].rearrange("(j p) f -> p j f", p=128), writes=[vtt[i]])
            K.dma(SP, sft[i].t[:], s.SfD[ti * cpt:(ti + 1) * cpt].rearrange("j p f -> p j f"), writes=[sft[i]])
            K.dma(SP, sbt[i].t[:], s.SbD[ti * cpt:(ti + 1) * cpt].rearrange("j p f -> p j f"), writes=[sbt[i]])

        load(0)
        chunks = [(ti, j) for ti in range(s.nt) for j in range(cpt)]

        def front(n):
            ti, j = chunks[n]
            i = ti % 2
            cs = slice(j * 128, (j + 1) * 128)
            P = Pb[n % 2]
            K.mm([(pSC.t[:, h * 128:(h + 1) * 128], kTt[i].t[:, h, cs], qTt[i].t[:, h, cs], True, True) for h in range(NH)],
                 reads=[kTt[i], qTt[i]], writes=[pSC])
            K.op(DVE, lambda e, P=P: e.tensor_tensor(P.t[:].rearrange("p h n -> p (h n)"), pSC.t[:],
                                                     MASK.t[:].rearrange("p h n -> p (h n)"), ALU.mult),
                 reads=[pSC, MASK], writes=[P])

        front(0)
        for n, (ti, j) in enumerate(chunks):
            if j == 0 and ti + 1 < s.nt:
                load(ti + 1)
            i = ti % 2
            sg_, vt_, sf_, sb_, mx = sgt[i], vtt[i], sft[i], sbt[i], mixst[i]
            qf_, qb_ = qfTt[i], qbTt[i]
            P, sqo, pO = Pb[n % 2], sqob[n % 2], pOb[n % 2]
            cs = slice(j * 128, (j + 1) * 128)
            mms = []
            for h in range(NH):
                hs = slice(h * 128, (h + 1) * 128)
                mms.append((pO.t[:, hs], vt_.t[:, j, hs], P.t[:, h, :], True, False))
                mms.append((pO.t[:, hs], sf_.t[:, j, hs], qf_.t[:, h, cs], False, False))
                mms.append((pO.t[:, hs], sb_.t[:, j, hs], qb_.t[:, h, cs], False, True))
            K.mm(mms, reads=[vt_, P, sf_, qf_, sb_, qb_], writes=[pO])
            if n + 1 < len(chunks):
                front(n + 1)
            K.op(ACT, lambda e, sqo=sqo, pO=pO: e.activation(sqo.t[:].rearrange("p h n -> p (h n)"), pO.t[:], AF.Square),
                 reads=[pO], writes=[sqo])
            K.mm([(pSS.t[:, h * 128:(h + 1) * 128], ones_bf.t[:], sqo.t[:, h, :], True, True) for h in range(NH)],
                 reads=[ones_bf, sqo], writes=[pSS])
            K.op(ACT, lambda e: e.activation(rt.t[:].rearrange("p h n -> p (h n)"), pSS.t[:], AF.Sqrt,
                                             bias=epsb.t[:], scale=1.0 / DH), reads=[pSS, epsb], writes=[rt])
            K.op(DVE, lambda e: e.reciprocal(rinv.t[:], rt.t[:]), reads=[rt], writes=[rinv])
            K.op(DVE, lambda e, pO=pO: e.tensor_tensor(on.t[:].rearrange("p h n -> p (h n)"), pO.t[:],
                                                       rinv.t[:].rearrange("p h n -> p (h n)"), ALU.mult),
                 reads=[pO, rinv], writes=[on])
            K.op(POOL, lambda e, mx=mx, sg_=sg_, cs=cs: e.tensor_tensor(mx.t[:, :, cs], on.t[:], sg_.t[:, :, cs], ALU.mult),
                 reads=[on, sg_], writes=[mx])
            if j == cpt - 1:
                K.dma(POOL, kview(s.mixT[0:RW, :], ti * TTs, TTs), mx.t[:], reads=[mx])
        K.barrier()
        st.close()

    def phase_C(s, l):
        st = ExitStack()
        K.new_phase()
        TTs = s.TT
        HL = CK // 2
        W = TTs + 2 * HL
        ut = [K.sb(st, f"C_u{i}", [128, 8, W], BF16, dma=True) for i in range(2)]
        acc = [K.sb(st, f"C_acc{c}", [128, TTs], F32) for c in range(8)]
        sq = K.sb(st, "C_sq", [128, 8, TTs], F32)
        mean = K.sb(st, "C_mean", [128, TTs], F32)
        msq = K.sb(st, "C_msq", [128, TTs], F32)
        var = K.sb(st, "C_var", [128, TTs], F32)
        rt = K.sb(st, "C_rt", [128, TTs], F32)
        rstd = K.sb(st, "C_rstd", [128, TTs], F32)
        tm = [K.sb(st, f"C_tm{i}", [128, TTs], F32) for i in range(2)]
        ost = [K.sb(st, f"C_o{i}", [128, 8, TTs], BF16, dma=True) for i in range(2)]
        pS1 = K.ps(st, "C_pS1", [128, 512], F32)
        pS2 = K.ps(st, "C_pS2", [128, 512], F32)
        pcv = [K.ps(st, f"C_pc{i}", [128, 512], F32) for i in range(2)]
        pc_i = [0]
        dg = K.sb(st, "C_dg", [128, 8, CK, 128], BF16)
        for c in range(8):
            for k in range(CK):
                E = DVE if (c * CK + k) % 2 == 0 else POOL
                K.op(E, lambda e, c=c, k=k: e.tensor_scalar_mul(dg.t[:, c, k, :], ident_bf.t[:], cdw_sb.t[:, c, k:k + 1]),
                     reads=[ident_bf, cdw_sb], writes=[dg])
        uview = s.uT.rearrange("(k p) t -> p k t", p=128)
        if not s.is_ctx:
            xs_ = K.sem("d_xch")
            for which, c0 in ((0, 0), (1, s.L - HL)):
                SP.e.dma_start(out=uhx[which * CW:(which + 1) * CW, 0:HL], in_=s.uT[:, c0:c0 + HL]).then_inc(xs_.h, 16)
                xs_.count += 16
            collective(uhx, uhg)

        def load(ti):
            u = ut[ti % 2]
            t0 = ti * TTs
            lo = max(t0 - HL, 0)
            hi = min(t0 + TTs + HL, s.L)
            if s.is_ctx and (lo != t0 - HL or hi != t0 + TTs + HL):
                K.op(POOL, lambda e, u=u: e.memset(u.t[:], 0.0), writes=[u])
            K.dma(SP, u.t[:, :, lo - (t0 - HL):hi - (t0 - HL)], uview[:, :, lo:hi], writes=[u])
            if not s.is_ctx:
                if ti == 0:
                    K.dma(SP, u.t[:, :, 0:HL], uhg[1 * CW:2 * CW, 0:HL].rearrange("(k p) t -> p k t", p=128), writes=[u])
                    K.op(DVE, lambda e, u=u: e.tensor_scalar_mul(u.t[:, :, 0:HL], u.t[:, :, 0:HL], flags_sb.t[:, 1:2]),
                         reads=[u, flags_sb], writes=[u])
                if ti == s.nt - 1:
                    K.dma(SP, u.t[:, :, W - HL:W], uhg[2 * CW:3 * CW, 0:HL].rearrange("(k p) t -> p k t", p=128), writes=[u])
                    K.op(DVE, lambda e, u=u: e.tensor_scalar_mul(u.t[:, :, W - HL:W], u.t[:, :, W - HL:W], flags_sb.t[:, 0:1]),
                         reads=[u, flags_sb], writes=[u])

        load(0)
        for ti in range(s.nt):
            if ti + 1 < s.nt:
                load(ti + 1)
            u = ut[ti % 2]
            o = ost[ti % 2]
            for c in range(8):
                pc = pcv[pc_i[0] % 2]
                pc_i[0] += 1
                K.mm([(pc.t[:, :TTs], dg.t[:, c, k, :], u.t[:, c, k:k + TTs], k == 0, k == CK - 1) for k in range(CK)],
                     reads=[dg, u], writes=[pc])
                K.op(ACT, lambda e, c=c, pc=pc: e.activation(acc[c].t[:], pc.t[:, :TTs], AF.Identity, bias=cdb_sb.t[:, c:c + 1], scale=1.0),
                     reads=[pc, cdb_sb], writes=[acc[c]])
            for c in range(8):
                K.op(ACT, lambda e, c=c: e.activation(sq.t[:, c, :], acc[c].t[:], AF.Square), reads=[acc[c]], writes=[sq])
            K.mm([(pS1.t[:, :TTs], ones_f.t[:], acc[c].t[:], c == 0, c == 7) for c in range(8)],
                 reads=[ones_f] + acc, writes=[pS1])
            K.mm([(pS2.t[:, :TTs], ones_f.t[:], sq.t[:, c, :], c == 0, c == 7) for c in range(8)],
                 reads=[ones_f, sq], writes=[pS2])
            K.op(ACT, lambda e: e.mul(mean.t[:], pS1.t[:, :TTs], 1.0 / CW), reads=[pS1], writes=[mean])
            K.op(DVE, lambda e: e.tensor_tensor(msq.t[:], mean.t[:], mean.t[:], ALU.mult), reads=[mean], writes=[msq])
            K.op(DVE, lambda e: e.scalar_tensor_tensor(var.t[:], pS2.t[:, :TTs], 1.0 / CW, msq.t[:], ALU.mult, ALU.subtract),
                 reads=[pS2, msq], writes=[var])
            K.op(DVE, lambda e: e.tensor_scalar_max(var.t[:], var.t[:], 0.0), reads=[var], writes=[var])
            K.op(ACT, lambda e: e.activation(rt.t[:], var.t[:], AF.Sqrt, bias=epsb.t[:], scale=1.0), reads=[var, epsb], writes=[rt])
            K.op(DVE, lambda e: e.reciprocal(rstd.t[:], rt.t[:]), reads=[rt], writes=[rstd])
            for c in range(8):
                t_ = tm[c % 2]
                K.op(DVE, lambda e, c=c, t_=t_: e.tensor_tensor(t_.t[:], acc[c].t[:], mean.t[:], ALU.subtract),
                     reads=[acc[c], mean], writes=[t_])
                K.op(POOL, lambda e, t_=t_: e.tensor_tensor(t_.t[:], t_.t[:], rstd.t[:], ALU.mult), reads=[t_, rstd], writes=[t_])
                K.op(ACT, lambda e, c=c, t_=t_, o=o: e.activation(o.t[:, c, :], t_.t[:], AF.Silu, bias=clb_sb.t[:, c:c + 1],
                                                                  scale=clg_sb.t[:, c:c + 1]),
                     reads=[t_, clb_sb, clg_sb], writes=[o])
            K.dma(POOL, kview(s.mixT[RW:D, :], ti * TTs, TTs), o.t[:], reads=[o])
        K.barrier()
        st.close()

    def phase_D(s, l):
        st = ExitStack()
        K.new_phase()
        par = l % 2
        TTs = s.TT
        xsrc = s.src if l == 0 else s.xs
        xs = K.sb(st, "D_x", [128, KC, TTs], F32, dma=True)
        mx = [K.sb(st, f"D_mix{i}", [128, KC, TTs], BF16, dma=True) for i in range(2)]
        hT = K.sb(st, "D_hT", [128, KC, TTs], BF16)
        sq = [K.sb(st, f"D_sq{i}", [128, 4, TTs], BF16) for i in range(2)]
        rt = K.sb(st, "D_rt", [128, TTs], F32)
        rstd = K.sb(st, "D_rstd", [128, TTs], F32)
        tmp = [K.sb(st, f"D_tmp{i}", [128, TTs], F32) for i in range(2)]
        wsl = [K.sb(st, f"D_w{i}", [128, KC, 512], BF16, dma=True) for i in range(3)]
        gst = [K.sb(st, f"D_gst{i}", [128, 4, TTs], BF16, dma=True) for i in range(2)]
        pss = K.ps(st, "D_pss", [128, 512], F32)
        pacc = [K.ps(st, f"D_pacc{i}", [128, 512], F32) for i in range(6)]
        acc_i = 0
        w_i = 0
        g_i = 0
        mview = s.mixT.rearrange("(k p) t -> p k t", p=128)
        K.dma(SP, mx[0].t[:], mview[:, :, 0:TTs], writes=[mx[0]])
        cvt_wait(l, "out")
        cvt_wait(l, "up")
        for ti in range(s.nt):
            t0 = ti * TTs
            m_ = mx[ti % 2]
            if not s.is_ctx:
                cvt_pump(3)
            K.dma(SP, xs.t[:], kview(xsrc, t0, TTs), writes=[xs])
            if ti + 1 < s.nt:
                K.dma(SP, mx[(ti + 1) % 2].t[:], mview[:, :, t0 + TTs:t0 + 2 * TTs], writes=[mx[(ti + 1) % 2]])
            for b in range(4):
                w = wsl[w_i % 3]
                w_i += 1
                K.dma(SP, w.t[:].rearrange("p k n -> p (k n)"), wb_out[par, b], writes=[w])
                for j in range(4):
                    oc = b * 4 + j
                    p = pacc[acc_i % 6]
                    acc_i += 1
                    K.mm([(p.t[:, :TTs], w.t[:, kc, j * 128:(j + 1) * 128], m_.t[:, kc, :], kc == 0, kc == KC - 1)
                          for kc in range(KC)], reads=[m_, w], writes=[p])
                    K.op(DVE, lambda e, p=p, oc=oc: e.scalar_tensor_tensor(
                        xs.t[:, oc, :], p.t[:, :TTs], modT.t[:, l, 32 + oc, s.v:s.v + 1], xs.t[:, oc, :], ALU.mult, ALU.add),
                        reads=[p, modT, xs], writes=[xs])
            K.dma(POOL, kview(s.xs, t0, TTs), xs.t[:], reads=[xs])
            norm_mod((sq, pss, rt, rstd, tmp), xs, hT, TTs, gs2, l, s.v, 48)
            for b in range(22):
                w = wsl[w_i % 3]
                w_i += 1
                K.dma(SP, w.t[:].rearrange("p k n -> p (k n)"), wb_up[par, b], writes=[w])
                g = gst[g_i % 2]
                g_i += 1
                for j in range(4):
                    p = pacc[acc_i % 6]
                    acc_i += 1
                    K.mm([(p.t[:, :TTs], w.t[:, kc, j * 128:(j + 1) * 128], hT.t[:, kc, :], kc == 0, kc == KC - 1)
                          for kc in range(KC)], reads=[hT, w], writes=[p])
                    if j % 2 == 0:
                        K.op(ACT, lambda e, p=p, g=g, j=j: e.copy(g.t[:, j, :], p.t[:, :TTs]), reads=[p], writes=[g])
                    else:
                        K.op(DVE, lambda e, p=p, g=g, j=j: e.tensor_copy(g.t[:, j, :], p.t[:, :TTs]), reads=[p], writes=[g])
                dst = s.gateT if b < 11 else s.valT
                bb = b % 11
                K.dma(POOL, dst[bb * 512:(bb + 1) * 512, t0:t0 + TTs].rearrange("(j p) t -> p j t", p=128), g.t[:], reads=[g])
        K.barrier()
        st.close()

    def phase_E1(s, l):
        st = ExitStack()
        K.new_phase()
        TTs = s.TT
        HL = GW
        W = TTs + 2 * HL
        dgs = [K.sb(st, f"E1_dg{i}", [128, 4, 9, 128], BF16) for i in range(2)]
        gsl = [K.sb(st, f"E1_g{i}", [128, 4, W], BF16, dma=True) for i in range(3)]
        vsl = [K.sb(st, f"E1_v{i}", [128, 4, TTs], BF16, dma=True) for i in range(3)]
        gls = [K.sb(st, f"E1_gl{i}", [128, 4, W + 2], BF16) for i in range(2)]
        grs = [K.sb(st, f"E1_gr{i}", [128, 4, W + 2], BF16) for i in range(2)]
        sls = [K.sb(st, f"E1_sl{i}", [128, TTs], F32) for i in range(2)]
        asts = [K.sb(st, f"E1_a{i}", [128, 4, TTs], BF16, dma=True) for i in range(2)]
        pcv = [K.ps(st, f"E1_pc{i}", [128, 512], F32) for i in range(4)]
        maskL = K.sb(st, "E1_mL", [128, W], BF16)
        maskR = K.sb(st, "E1_mR", [128, W], BF16)
        gview = s.gateT.rearrange("(k p) t -> p k t", p=128)
        vview = s.valT.rearrange("(k p) t -> p k t", p=128)
        if not s.is_ctx:
            xs_ = K.sem("d_xch")
            for which, c0 in ((0, 0), (1, s.L - HL)):
                for q4 in range(4):
                    r0, r1 = q4 * (DFF // 4), (q4 + 1) * (DFF // 4)
                    SP.e.dma_start(out=ghx[which * DFF + r0:which * DFF + r1, :], in_=s.gateT[r0:r1, c0:c0 + HL]).then_inc(xs_.h, 16)
                    xs_.count += 16
            collective(ghx, ghg)
            for b_ in gls + grs:
                K.op(POOL, lambda e, b_=b_: e.memset(b_.t[:], 0.0), writes=[b_])
            K.op(DVE, lambda e: e.memset(maskL.t[:], 1.0), writes=[maskL])
            K.op(DVE, lambda e: e.memset(maskR.t[:], 1.0), writes=[maskR])
            K.op(DVE, lambda e: e.memset(maskL.t[:].rearrange("p (r w) -> p r w", w=GW)[:, :, GW - 1:GW], 0.0), writes=[maskL])
            K.op(DVE, lambda e: e.memset(maskR.t[:].rearrange("p (r w) -> p r w", w=GW)[:, :, 0:1], 0.0), writes=[maskR])
        if s.is_ctx:
            taps = [(3, -1, 0), (4, 0, 0), (5, 1, 0)]
        else:
            taps = [((dr + 1) * 3 + (dc + 1), dr * GW + dc, dc) for dr in (-1, 0, 1) for dc in (-1, 0, 1)]
        items = [(g, ti) for g in range(11) for ti in range(s.nt)]

        def load(n):
            g, ti = items[n]
            t0 = ti * TTs
            gb = gsl[n % 3]
            vb = vsl[n % 3]
            lo = max(t0 - HL, 0)
            hi = min(t0 + TTs + HL, s.L)
            if s.is_ctx and (lo != t0 - HL or hi != t0 + TTs + HL):
                K.op(POOL, lambda e, gb=gb: e.memset(gb.t[:], 0.0), writes=[gb])
            K.dma(SP, gb.t[:, :, lo - (t0 - HL):hi - (t0 - HL)], gview[:, g * 4:(g + 1) * 4, lo:hi], writes=[gb])
            if not s.is_ctx:
                if ti == 0:
                    r0 = 1 * DFF + g * 512
                    K.dma(SP, gb.t[:, :, 0:HL], ghg[r0:r0 + 512, :].rearrange("(k p) t -> p k t", p=128), writes=[gb])
                    K.op(DVE, lambda e, gb=gb: e.tensor_scalar_mul(gb.t[:, :, 0:HL], gb.t[:, :, 0:HL], flags_sb.t[:, 1:2]),
                         reads=[gb, flags_sb], writes=[gb])
                if ti == s.nt - 1:
                    r0 = 2 * DFF + g * 512
                    K.dma(SP, gb.t[:, :, W - HL:W], ghg[r0:r0 + 512, :].rearrange("(k p) t -> p k t", p=128), writes=[gb])
                    K.op(DVE, lambda e, gb=gb: e.tensor_scalar_mul(gb.t[:, :, W - HL:W], gb.t[:, :, W - HL:W], flags_sb.t[:, 0:1]),
                         reads=[gb, flags_sb], writes=[gb])
            K.dma(SP, vb.t[:], vview[:, g * 4:(g + 1) * 4, t0:t0 + TTs], writes=[vb])

        def build_dg(g):
            dg = dgs[g % 2]
            for jj in range(4):
                fc = g * 4 + jj
                for (wi, off, sel) in taps:
                    E = DVE if (jj + wi) % 2 == 0 else POOL
                    K.op(E, lambda e, jj=jj, wi=wi, fc=fc, dg=dg: e.tensor_scalar_mul(dg.t[:, jj, wi, :], ident_bf.t[:], fdw_sb.t[:, fc, wi:wi + 1]),
                         reads=[ident_bf, fdw_sb], writes=[dg])

        def prep(n):
            if s.is_ctx:
                return
            gb = gsl[n % 3]
            gl = gls[n % 2]
            gr = grs[n % 2]
            K.op(ACT, lambda e, gl=gl, gb=gb: e.copy(gl.t[:, :, 1:1 + W], gb.t[:]), reads=[gb], writes=[gl])
            K.op(DVE, lambda e, gl=gl: e.memset(gl.t[:, :, 1:1 + W].rearrange("p j (r w) -> p j r w", w=GW)[:, :, :, GW - 1:GW], 0.0),
                 reads=[gl], writes=[gl])
            K.op(ACT, lambda e, gr=gr, gb=gb: e.copy(gr.t[:, :, 1:1 + W], gb.t[:]), reads=[gb], writes=[gr])
            K.op(DVE, lambda e, gr=gr: e.memset(gr.t[:, :, 1:1 + W].rearrange("p j (r w) -> p j r w", w=GW)[:, :, :, 0:1], 0.0),
                 reads=[gr], writes=[gr])

        for n in range(min(2, len(items))):
            load(n)
        build_dg(0)
        prep(0)
        pc_i = 0
        for n, (g, ti) in enumerate(items):
            if n + 2 < len(items):
                load(n + 2)
            if n + 1 < len(items):
                prep(n + 1)
            t0 = ti * TTs
            dg = dgs[g % 2]
            gb = gsl[n % 3]
            vb = vsl[n % 3]
            gl = gls[n % 2]
            gr = grs[n % 2]
            ast = asts[n % 2]
            for jj in range(4):
                fc = g * 4 + jj
                pc = pcv[pc_i % 4]
                sl = sls[pc_i % 2]
                pc_i += 1
                mms = []
                for ix, (wi, off, sel) in enumerate(taps):
                    if sel == 0 or s.is_ctx:
                        rhs = gb.t[:, jj, HL + off:HL + off + TTs]
                    else:
                        src = gl if sel < 0 else gr
                        rhs = src.t[:, jj, 1 + HL + off:1 + HL + off + TTs]
                    mms.append((pc.t[:, :TTs], dg.t[:, jj, wi, :], rhs, ix == 0, ix == len(taps) - 1))
                K.mm(mms, reads=[dg, gb] + ([] if s.is_ctx else [gl, gr]), writes=[pc])
                K.op(ACT, lambda e, pc=pc, sl=sl, fc=fc: e.activation(sl.t[:], pc.t[:, :TTs], AF.Silu, bias=fdb_sb.t[:, fc:fc + 1], scale=1.0),
                     reads=[pc, fdb_sb], writes=[sl])
                K.op(DVE, lambda e, sl=sl, vb=vb, jj=jj, ast=ast: e.tensor_tensor(ast.t[:, jj, :], sl.t[:], vb.t[:, jj, :], ALU.mult),
                     reads=[sl, vb], writes=[ast])
            K.dma(POOL, s.actT[g * 512:(g + 1) * 512, t0:t0 + TTs].rearrange("(j p) t -> p j t", p=128), ast.t[:], reads=[ast])
            if ti == 0 and g + 1 < 11:
                build_dg(g + 1)
        K.barrier()
        st.close()

    def phase_E2(s, l):
        st = ExitStack()
        K.new_phase()
        par = l % 2
        TTs = s.TT
        xo = [K.sb(st, f"E_xo{i}", [128, TTs], F32, dma=True) for i in range(3)]
        act = [K.sb(st, f"E_act{i}", [128, FC, TTs], BF16, dma=True) for i in range(2)]
        wsl = [K.sb(st, f"E_w{i}", [128, FC, 128], BF16, dma=True) for i in range(3)]
        pacc = [K.ps(st, f"E_pacc{i}", [128, 512], F32) for i in range(4)]
        aview = s.actT.rearrange("(k p) t -> p k t", p=128)
        cvt_wait(l, "dn")
        K.dma(SP, act[0].t[:], aview[:, :, 0:TTs], writes=[act[0]])
        acc_i = 0
        w_i = 0
        for ti in range(s.nt):
            t0 = ti * TTs
            a_ = act[ti % 2]
            if ti + 1 < s.nt:
                K.dma(SP, act[(ti + 1) % 2].t[:], aview[:, :, t0 + TTs:t0 + 2 * TTs], writes=[act[(ti + 1) % 2]])
            if not s.is_ctx:
                cvt_pump(1)
            for oc in range(KC):
                w = wsl[w_i % 3]
                x_ = xo[w_i % 3]
                w_i += 1
                K.dma(SP, w.t[:].rearrange("p k n -> p (k n)"), wb_dn[par, oc], writes=[w])
                K.dma(SP, x_.t[:], s.xs[oc * 128:(oc + 1) * 128, t0:t0 + TTs], writes=[x_])
                p = pacc[acc_i % 4]
                acc_i += 1
                K.mm([(p.t[:, :TTs], w.t[:, fc, :], a_.t[:, fc, :], fc == 0, fc == FC - 1)
                      for fc in range(FC)], reads=[a_, w], writes=[p])
                K.op(DVE, lambda e, p=p, oc=oc, x_=x_: e.scalar_tensor_tensor(
                    x_.t[:], p.t[:, :TTs], modT.t[:, l, 80 + oc, s.v:s.v + 1], x_.t[:], ALU.mult, ALU.add),
                    reads=[p, modT, x_], writes=[x_])
                K.dma(POOL, s.xs[oc * 128:(oc + 1) * 128, t0:t0 + TTs], x_.t[:], reads=[x_])
        K.barrier()
        st.close()

    def phase_final(s):
        st = ExitStack()
        K.new_phase()
        TTs = s.TT
        xsl = [K.sb(st, f"F_x{i}", [128, KC, TTs], F32, dma=True) for i in range(2)]
        osl = [K.sb(st, f"F_o{i}", [128, KC, TTs], F32, dma=True) for i in range(2)]
        sq = K.sb(st, "F_sq", [128, KC, TTs], BF16)
        rt = K.sb(st, "F_rt", [128, TTs], F32)
        rstd = K.sb(st, "F_rstd", [128, TTs], F32)
        pss = K.ps(st, "F_pss", [128, 512], F32)
        K.dma(SP, xsl[0].t[:], kview(s.xs, 0, TTs), writes=[xsl[0]])
        for ti in range(s.nt):
            t0 = ti * TTs
            xs = xsl[ti % 2]
            o = osl[ti % 2]
            if ti + 1 < s.nt:
                K.dma(SP, xsl[(ti + 1) % 2].t[:], kview(s.xs, t0 + TTs, TTs), writes=[xsl[(ti + 1) % 2]])
            for g4 in range(4):
                K.op(ACT, lambda e, g4=g4, xs=xs: e.activation(sq.t[:, g4 * 4:(g4 + 1) * 4, :], xs.t[:, g4 * 4:(g4 + 1) * 4, :], AF.Square),
                     reads=[xs], writes=[sq])
            K.mm([(pss.t[:, :TTs], ones_bf.t[:], sq.t[:, kc, :], kc == 0, kc == KC - 1) for kc in range(KC)],
                 reads=[ones_bf, sq], writes=[pss])
            K.op(ACT, lambda e: e.activation(rt.t[:], pss.t[:, :TTs], AF.Sqrt, bias=epsb.t[:], scale=1.0 / D),
                 reads=[pss, epsb], writes=[rt])
            K.op(DVE, lambda e: e.reciprocal(rstd.t[:], rt.t[:]), reads=[rt], writes=[rstd])
            for kc in range(KC):
                E = DVE
                K.op(E, lambda e, kc=kc, xs=xs, o=o: e.scalar_tensor_tensor(
                    o.t[:, kc, :], xs.t[:, kc, :], fng_sb.t[:, kc:kc + 1], rstd.t[:], ALU.mult, ALU.mult),
                    reads=[xs, fng_sb, rstd], writes=[o])
            K.dma(SP, kview(outT, t0, TTs), o.t[:], reads=[o])
        K.barrier()
        st.close()

    convert_weights(0)
    cvt_pump(12)
    mod_phase()
    for l in range(depth):
        last = l == depth - 1
        if l + 1 < depth:
            convert_weights(l + 1)
        layer_params(l)
        K.barrier()
        for s in (sctx, slat):
            phase_A(s, l)
            phase_B1(s, l)
            phase_B2(s, l)
            if s.is_ctx and last:
                continue
            phase_C(s, l)
            phase_D(s, l)
            phase_E1(s, l)
            phase_E2(s, l)
    phase_final(slat)
    K.stack.close()
    return nc


def _fm(vec, nk):
    return np.ascontiguousarray(np.asarray(vec, np.float32).reshape(nk, 128).T)


def _blk(w, ncols_blk):
    Kd, N = w.shape
    kc = Kd // 128
    nb = N // ncols_blk
    a = w.reshape(kc, 128, nb, ncols_blk).transpose(2, 1, 0, 3)
    return np.ascontiguousarray(a).reshape(nb, 128, kc * ncols_blk)


def prepare_shared(inputs, depth, L):
    sh = {}
    w_mod = np.asarray(inputs["w_mod"], np.float32)
    sh["w_mod"] = np.stack([_blk(w_mod[l], 512).reshape(24, 128, KC, 512) for l in range(depth)])
    sh["b_mod"] = np.stack([_fm(inputs["b_mod"][l], 96) for l in range(depth)])
    sh["norm1_g"] = np.stack([_fm(inputs["norm1_g"][l], KC) for l in range(depth)])
    sh["norm2_g"] = np.stack([_fm(inputs["norm2_g"][l], KC) for l in range(depth)])
    w_in = np.asarray(inputs["w_in"], np.float32)
    wl = []
    for l in range(depth):
        w = w_in[l]
        cols = [w[:, b * 512:(b + 1) * 512] for b in range(8)]
        for pp in range(4):
            a = w[:, 4096 + pp * 256:4096 + (pp + 1) * 256]
            bg = w[:, 5120 + pp * 256:5120 + (pp + 1) * 256]
            cols.append(np.concatenate([a, bg], axis=1))
        wl.append(_blk(np.concatenate(cols, axis=1), 512))
    sh["w_in"] = np.stack(wl)
    sh["w_out"] = np.stack([_blk(np.asarray(inputs["w_out"][l], np.float32), 512) for l in range(depth)])
    sh["w_up"] = np.stack([_blk(np.asarray(inputs["ffn_w_up"][l], np.float32), 512) for l in range(depth)])
    sh["w_down"] = np.stack([_blk(np.asarray(inputs["ffn_w_down"][l], np.float32), 128) for l in range(depth)])
    sh["ret_decay_f"] = np.ascontiguousarray(np.asarray(inputs["ret_decay_f"], np.float32)[:depth])
    sh["ret_decay_b"] = np.ascontiguousarray(np.asarray(inputs["ret_decay_b"], np.float32)[:depth])
    cw = np.asarray(inputs["conv_dw_w"], np.float32)
    sh["conv_dw_w"] = np.stack([np.ascontiguousarray(cw[l].T.reshape(8, 128, CK).transpose(1, 0, 2)) for l in range(depth)])
    sh["conv_dw_b"] = np.stack([_fm(inputs["conv_dw_b"][l], 8) for l in range(depth)])
    sh["conv_ln_g"] = np.stack([_fm(inputs["conv_ln_g"][l], 8) for l in range(depth)])
    sh["conv_ln_b"] = np.stack([_fm(inputs["conv_ln_b"][l], 8) for l in range(depth)])
    fw = np.asarray(inputs["ffn_dw_w"], np.float32)
    sh["ffn_dw_w"] = np.stack([np.ascontiguousarray(fw[l].reshape(9, FC, 128).transpose(2, 1, 0)) for l in range(depth)])
    sh["ffn_dw_b"] = np.stack([_fm(inputs["ffn_dw_b"][l], FC) for l in range(depth)])
    sh["final_norm_g"] = _fm(inputs["final_norm_g"], KC)
    rc, rs = _rope_tables(L)
    sh["rope_c"] = rc
    sh["rope_s"] = rs
    sh.update(_consts())
    return sh


def run(inputs, depth, ncores, trace=False):
    x = np.asarray(inputs["x"], np.float32)
    B, L, _ = x.shape
    assert 2 * B == ncores
    Lc = L // 2
    nc = build_program(Lc, depth, ncores)
    sh = prepare_shared(inputs, depth, L)
    rc, rs = sh.pop("rope_c"), sh.pop("rope_s")
    c = np.asarray(inputs["c"], np.float32)
    cc = np.asarray(inputs["c_ctx"], np.float32)
    ctx = np.asarray(inputs["ctx"], np.float32)
    in_maps = []
    for r in range(ncores):
        b, half = r // 2, r % 2
        m = dict(sh)
        sl = slice(half * Lc, (half + 1) * Lc)
        m["xT"] = np.ascontiguousarray(x[b, sl].T)
        m["ctxT"] = np.ascontiguousarray(ctx[b].T)
        m["cvec"] = np.ascontiguousarray(np.stack([_fm(c[b], KC), _fm(cc, KC)], axis=-1))
        m["rope_c"] = np.ascontiguousarray(rc[:, sl])
        m["rope_s"] = np.ascontiguousarray(rs[:, sl])
        fl = np.zeros((128, 2), np.float32)
        fl[:, half] = 1.0
        m["flags"] = fl
        in_maps.append(m)
    res = run_bass_kernel_spmd(nc, in_maps, core_ids=list(range(ncores)), **({"trace": True} if trace else {}))
    out = np.empty((B, L, D), np.float32)
    for r in range(ncores):
        b, half = r // 2, r % 2
        out[b, half * Lc:(half + 1) * Lc] = res.results[r]["outT"].T
    return out, res


def kernel(**inputs):
    out, _ = run(inputs, 4, 8)
    return out
```

```python
import numpy as np
from contextlib import ExitStack
import concourse.bass as bass
import concourse.mybir as mybir
from concourse.bass_utils import run_bass_kernel_spmd

F32 = mybir.dt.float32
BF16 = mybir.dt.bfloat16
AF = mybir.ActivationFunctionType
ALU = mybir.AluOpType

D = 2048
KC = 16
NH = 8
DH = 128
RW = 1024
CW = 1024
DFF = 5632
FC = 44
CTX = 256
GW = 64
CK = 31
EPS = 1e-6
THETA = 10000.0
NMOD = 6


class Sem:
    def __init__(self, h, name):
        self.h = h
        self.name = name
        self.count = 0


class Eng:
    def __init__(self, name, e, sem):
        self.name = name
        self.e = e
        self.sem = sem
        self.waited = {}


class Buf:
    def __init__(self, t, dsem=None):
        self.t = t
        self.w = None
        self.r = {}
        self.dsem = dsem

    def __getitem__(self, idx):
        return self.t[idx]


class KB:
    def __init__(self, nc):
        self.nc = nc
        self.stack = ExitStack()
        self.uid = 0
        self.sems = {}
        self.pe = self._eng("pe", nc.tensor)
        self.act = self._eng("act", nc.scalar)
        self.dve = self._eng("dve", nc.vector)
        self.pool = self._eng("pool", nc.gpsimd)
        self.sp = self._eng("sp", nc.sync)
        self.engs = [self.pe, self.act, self.dve, self.pool, self.sp]
        self.nobarrier = set()
        self.dpool_i = 0
        self.dpool_base = 0

    def new_phase(self):
        self.dpool_i = self.dpool_base

    def sem(self, name):
        if name not in self.sems:
            h = self.stack.enter_context(self.nc.semaphore(name))
            self.sems[name] = Sem(h, name)
        return self.sems[name]

    def _eng(self, name, e):
        return Eng(name, e, self.sem("eng_" + name))

    def sb(self, st, name, shape, dtype, dma=False):
        self.uid += 1
        t = st.enter_context(self.nc.sbuf_tensor(f"{name}_{self.uid}", list(shape), dtype))
        dsem = None
        if dma:
            dsem = self.sem(f"dslot{self.dpool_i}")
            self.dpool_i += 1
        return Buf(t, dsem)

    def ps(self, st, name, shape, dtype):
        self.uid += 1
        t = st.enter_context(self.nc.psum_tensor(f"{name}_{self.uid}", list(shape), dtype))
        return Buf(t)

    def _wait(self, E, deps):
        need = {}
        for d in deps:
            if d is None:
                continue
            S, v = d
            if need.get(S, 0) < v:
                need[S] = v
        for S, v in need.items():
            if E.waited.get(S, 0) < v:
                E.e.wait_ge(S.h, v)
                E.waited[S] = v

    def _deps(self, reads, writes):
        deps = []
        for b in reads:
            deps.append(b.w)
        for b in writes:
            deps.append(b.w)
            deps.extend(b.r.items())
        return deps

    def _commit(self, tok, reads, writes):
        S, v = tok
        for b in writes:
            b.w = tok
            b.r = {}
        for b in reads:
            if b.r.get(S, 0) < v:
                b.r[S] = v

    def op(self, E, fn, reads=(), writes=()):
        self._wait(E, self._deps(reads, writes))
        ins = fn(E.e)
        E.sem.count += 1
        ins.then_inc(E.sem.h, 1)
        self._commit((E.sem, E.sem.count), reads, writes)

    def mm(self, outs, reads, writes):
        E = self.pe
        self._wait(E, self._deps(reads, writes))
        ins = None
        for (o, l, r, s0, s1) in outs:
            ins = E.e.matmul(o, l, r, start=s0, stop=s1)
        E.sem.count += 1
        ins.then_inc(E.sem.h, 1)
        self._commit((E.sem, E.sem.count), reads, writes)

    def tr(self, outs, ident, reads, writes):
        E = self.pe
        self._wait(E, self._deps(list(reads) + [ident], writes))
        ins = None
        for (o, i) in outs:
            ins = E.e.transpose(o, i, ident.t[:])
        E.sem.count += 1
        ins.then_inc(E.sem.h, 1)
        self._commit((E.sem, E.sem.count), list(reads) + [ident], writes)

    def dma(self, Q, out, in_, reads=(), writes=(), sem=None):
        if sem is None:
            for b in list(writes) + list(reads):
                if b.dsem is not None:
                    sem = b.dsem
                    break
        assert sem is not None
        self._wait(Q, self._deps(reads, writes))
        Q.e.dma_start(out=out, in_=in_).then_inc(sem.h, 16)
        sem.count += 16
        self._commit((sem, sem.count), reads, writes)

    def barrier(self):
        for E in self.engs:
            for S in self.sems.values():
                if S.name in self.nobarrier:
                    continue
                if S.count > 0 and E.waited.get(S, 0) < S.count:
                    E.e.wait_ge(S.h, S.count)
                    E.waited[S] = S.count


def _rope_tables(L):
    t = np.arange(L)
    inv = 1.0 / (THETA ** (np.arange(32, dtype=np.float32) / 32.0)).astype(np.float32)
    inv = inv.astype(np.float32)
    ang_r = (t // GW).astype(np.float32)[:, None] * inv[None, :]
    ang_c = (t % GW).astype(np.float32)[:, None] * inv[None, :]
    cos = np.zeros((128, L), np.float32)
    sins = np.zeros((128, L), np.float32)
    cr, sr = np.cos(ang_r).T, np.sin(ang_r).T
    cc, sc = np.cos(ang_c).T, np.sin(ang_c).T
    cos[0:32], cos[32:64], cos[64:96], cos[96:128] = cr, cr, cc, cc
    sins[0:32], sins[32:64], sins[64:96], sins[96:128] = -sr, sr, -sc, sc
    return cos, sins


def _consts():
    c = {}
    idx = np.arange(128, dtype=np.float32)
    diff = idx[None, :] - idx[:, None]
    c["c_dpos"] = np.maximum(diff, 0.0).astype(np.float32)
    c["c_dneg"] = np.maximum(-diff, 0.0).astype(np.float32)
    c["c_ge"] = (diff >= 0).astype(np.float32)
    c["c_le"] = (diff <= 0).astype(np.float32)
    c["c_n1"] = np.broadcast_to(idx[None, :] + 1.0, (128, 128)).astype(np.float32).copy()
    c["c_cn"] = np.broadcast_to(128.0 - idx[None, :], (128, 128)).astype(np.float32).copy()
    c["c_kf"] = np.broadcast_to((127.0 - idx)[:, None], (128, 128)).astype(np.float32).copy()
    c["c_kb"] = np.broadcast_to(idx[:, None], (128, 128)).astype(np.float32).copy()
    perm = np.zeros((128, 128), np.float32)
    for p in range(128):
        q = p + 32 if (p % 64) < 32 else p - 32
        perm[q, p] = 1.0
    c["c_perm"] = perm
    c["c_ident"] = np.eye(128, dtype=np.float32)
    return c


CONST_NAMES = ["c_dpos", "c_dneg", "c_ge", "c_le", "c_n1", "c_cn", "c_kf", "c_kb", "c_perm", "c_ident"]


class Seq:
    pass


def build_program(L, depth, ncores=8, TT=512):
    nc = bass.Bass("TRN2", target_bir_lowering=False)
    K = KB(nc)

    def din(name, shape, dt=F32):
        return nc.dram_tensor(name, list(shape), dt, kind="ExternalInput").ap()

    def dint(name, shape, dt):
        return nc.dram_tensor(name, list(shape), dt, kind="Internal").ap()

    xT_in = din("xT", [D, L])
    cxT_in = din("ctxT", [D, CTX])
    cvec = din("cvec", [128, KC, 2])
    wmod = din("w_mod", [depth, 24, 128, KC, 512])
    bmod = din("b_mod", [depth, 128, 96])
    n1g = din("norm1_g", [depth, 128, KC])
    n2g = din("norm2_g", [depth, 128, KC])
    w_in = din("w_in", [depth, 12, 128, KC * 512])
    w_out = din("w_out", [depth, 4, 128, KC * 512])
    w_up = din("w_up", [depth, 22, 128, KC * 512])
    w_dn = din("w_down", [depth, 16, 128, FC * 128])
    decf = din("ret_decay_f", [depth, NH])
    decb = din("ret_decay_b", [depth, NH])
    cdw = din("conv_dw_w", [depth, 128, 8, CK])
    cdb = din("conv_dw_b", [depth, 128, 8])
    clg = din("conv_ln_g", [depth, 128, 8])
    clb = din("conv_ln_b", [depth, 128, 8])
    fdw = din("ffn_dw_w", [depth, 128, FC, 9])
    fdb = din("ffn_dw_b", [depth, 128, FC])
    fng = din("final_norm_g", [128, KC])
    flags_in = din("flags", [128, 2])
    ropec = din("rope_c", [128, L])
    ropes = din("rope_s", [128, L])
    cst_in = {n: din(n, [128, 128]) for n in CONST_NAMES}
    outT = nc.dram_tensor("outT", [D, L], F32, kind="ExternalOutput").ap()

    wb_in = dint("wb_in", [2, 12, 128, KC * 512], BF16)
    wb_out = dint("wb_out", [2, 4, 128, KC * 512], BF16)
    wb_up = dint("wb_up", [2, 22, 128, KC * 512], BF16)
    wb_dn = dint("wb_dn", [2, 16, 128, FC * 128], BF16)

    stx = dint("stx", [256, RW], F32)
    stg = dint("stg", [512, RW], F32)
    uhx = dint("uhx", [2 * CW, 16], BF16)
    uhg = dint("uhg", [4 * CW, 16], BF16)
    ghx = dint("ghx", [2 * DFF, GW], BF16)
    ghg = dint("ghg", [4 * DFF, GW], BF16)
    seqs = []
    for nm, Ls, src, rope in (("c", CTX, cxT_in, False), ("x", L, xT_in, True)):
        s = Seq()
        s.name = nm
        s.L = Ls
        s.src = src
        s.rope = rope
        s.is_ctx = not rope
        s.TT = min(TT, Ls)
        s.nt = Ls // s.TT
        s.cpt = s.TT // 128
        s.nch = Ls // 128
        s.xs = dint(f"xs_{nm}", [D, Ls], F32)
        s.qT = dint(f"qT_{nm}", [RW, Ls], BF16)
        s.kT = dint(f"kT_{nm}", [RW, Ls], BF16)
        s.qfT = dint(f"qfT_{nm}", [RW, Ls], BF16)
        s.qbT = dint(f"qbT_{nm}", [RW, Ls], BF16)
        s.ktok = dint(f"ktok_{nm}", [Ls, RW], BF16)
        s.vtok = dint(f"vtok_{nm}", [Ls, RW], BF16)
        s.sgT = dint(f"sgT_{nm}", [RW, Ls], BF16)
        s.uT = dint(f"uT_{nm}", [CW, Ls], BF16)
        s.SbD = dint(f"SbD_{nm}", [Ls // 128, 128, RW], BF16)
        s.SfD = dint(f"SfD_{nm}", [Ls // 128, 128, RW], BF16)
        s.mixT = dint(f"mixT_{nm}", [D, Ls], BF16)
        s.gateT = dint(f"gateT_{nm}", [DFF, Ls], BF16)
        s.valT = dint(f"valT_{nm}", [DFF, Ls], BF16)
        s.actT = dint(f"actT_{nm}", [DFF, Ls], BF16)
        s.v = 1 if s.is_ctx else 0
        seqs.append(s)
    sctx, slat = seqs

    top = K.stack
    SP, ACT, DVE, POOL, PE = K.sp, K.act, K.dve, K.pool, K.pe

    cst = {n: K.sb(top, n, [128, 128], F32, dma=True) for n in CONST_NAMES}
    modT = K.sb(top, "modT", [128, depth, 96, 2], F32)
    gs1 = K.sb(top, "gs1", [128, depth, KC, 2], F32)
    gs2 = K.sb(top, "gs2", [128, depth, KC, 2], F32)
    ones_bf = K.sb(top, "ones_bf", [128, 128], BF16)
    ones_f = K.sb(top, "ones_f", [128, 128], F32)
    ident_bf = K.sb(top, "ident_bf", [128, 128], BF16)
    epsb = K.sb(top, "epsb", [128, 1], F32)
    sfin = K.sb(top, "sfin", [128, RW], F32)
    sbin = K.sb(top, "sbin", [128, RW], F32)
    n1g_sb = K.sb(top, "n1g", [128, depth, KC], F32, dma=True)
    n2g_sb = K.sb(top, "n2g", [128, depth, KC], F32, dma=True)
    bmod_sb = K.sb(top, "bmod", [128, depth, 96], F32, dma=True)
    cvec_sb = K.sb(top, "cvec", [128, KC, 2], F32, dma=True)
    fng_sb = K.sb(top, "fng", [128, KC], F32, dma=True)
    cdw_sb = K.sb(top, "cdw", [128, 8, CK], F32, dma=True)
    cdb_sb = K.sb(top, "cdb", [128, 8], F32, dma=True)
    clg_sb = K.sb(top, "clg", [128, 8], F32, dma=True)
    clb_sb = K.sb(top, "clb", [128, 8], F32, dma=True)
    fdw_sb = K.sb(top, "fdw", [128, FC, 9], F32, dma=True)
    fdb_sb = K.sb(top, "fdb", [128, FC], F32, dma=True)
    dec_sb = K.sb(top, "dec", [128, 2, NH], F32, dma=True)
    lg_sb = K.sb(top, "lg", [128, 2, NH], F32)
    MASK = K.sb(top, "MASK", [128, NH, 128], F32)
    QDF = K.sb(top, "QDF", [128, NH, 128], F32)
    QDB = K.sb(top, "QDB", [128, NH, 128], F32)
    KDF = K.sb(top, "KDF", [128, NH], F32)
    KDB = K.sb(top, "KDB", [128, NH], F32)
    CDF = K.sb(top, "CDF", [128, NH], F32)
    CDB = K.sb(top, "CDB", [128, NH], F32)

    flags_sb = K.sb(top, "flags", [128, 2], F32, dma=True)
    pairs = [[2 * i, 2 * i + 1] for i in range(ncores // 2)]
    ccs = K.sem("ccs")

    def collective(src, dst):
        K.barrier()
        POOL.e.collective_compute("AllGather", ALU.bypass, replica_groups=pairs, ins=[src], outs=[dst]).then_inc(ccs.h)
        ccs.count += 1
        K.barrier()

    K.dpool_base = K.dpool_i
    cvt_sems = {n: K.sem("cvt_" + n) for n in ("in", "out", "up", "dn")}
    for n in cvt_sems:
        K.nobarrier.add("cvt_" + n)
    cvt_need = {}
    cvt_queue = []

    def convert_weights(l):
        par = l % 2
        cnt = {n: cvt_sems[n].count + sum(16 for q in cvt_queue if q[0] == n) for n in cvt_sems}
        for (n, src, dst, nb) in (("in", w_in, wb_in, 12), ("out", w_out, wb_out, 4), ("up", w_up, wb_up, 22), ("dn", w_dn, wb_dn, 16)):
            for b in range(nb):
                cvt_queue.append((n, dst[par, b], src[l, b]))
            cnt[n] += 16 * nb
            cvt_need[(l, n)] = cnt[n]

    def cvt_pump(k):
        for _ in range(k):
            if not cvt_queue:
                return
            n, dst, src = cvt_queue.pop(0)
            sm = cvt_sems[n]
            POOL.e.dma_start(out=dst, in_=src).then_inc(sm.h, 16)
            sm.count += 16

    def cvt_wait(l, n):
        sm = cvt_sems[n]
        while sm.count < cvt_need[(l, n)]:
            cvt_pump(1)
        for E in K.engs:
            E.e.wait_ge(sm.h, cvt_need[(l, n)])

    def mod_phase():
        st = ExitStack()
        K.new_phase()
        for n in CONST_NAMES:
            K.dma(SP, cst[n].t[:], cst_in[n], writes=[cst[n]])
        K.dma(SP, n1g_sb.t[:], n1g.rearrange("l p k -> p l k"), writes=[n1g_sb])
        K.dma(SP, n2g_sb.t[:], n2g.rearrange("l p k -> p l k"), writes=[n2g_sb])
        K.dma(SP, bmod_sb.t[:], bmod.rearrange("l p k -> p l k"), writes=[bmod_sb])
        K.dma(SP, cvec_sb.t[:], cvec, writes=[cvec_sb])
        K.dma(SP, fng_sb.t[:], fng, writes=[fng_sb])
        K.dma(SP, flags_sb.t[:], flags_in, writes=[flags_sb])
        K.op(DVE, lambda e: e.memset(ones_bf.t[:], 1.0), writes=[ones_bf])
        K.op(DVE, lambda e: e.memset(ones_f.t[:], 1.0), writes=[ones_f])
        K.op(DVE, lambda e: e.memset(epsb.t[:], EPS), writes=[epsb])
        K.op(DVE, lambda e: e.tensor_copy(ident_bf.t[:], cst["c_ident"].t[:]), reads=[cst["c_ident"]], writes=[ident_bf])
        sc = K.sb(st, "silu_c", [128, KC, 2], F32)
        K.op(ACT, lambda e: e.activation(sc.t[:], cvec_sb.t[:], AF.Silu), reads=[cvec_sb], writes=[sc])
        wsl = [K.sb(st, f"wmod{i}", [128, KC, 512], F32, dma=True) for i in range(3)]
        pm = [K.ps(st, f"pmod{i}", [128, 8], F32) for i in range(2)]
        blocks = [(l, b) for l in range(depth) for b in range(24)]
        PF = 2

        def load(i):
            l, b = blocks[i]
            w = wsl[i % 3]
            K.dma(SP, w.t[:], wmod[l, b], writes=[w])

        for i in range(min(PF, len(blocks))):
            load(i)
        for i, (l, b) in enumerate(blocks):
            if i + PF < len(blocks):
                load(i + PF)
            w = wsl[i % 3]
            p = pm[i % 2]
            for j in range(4):
                K.mm([(p.t[:, 2 * j:2 * j + 2], w.t[:, kc, j * 128:(j + 1) * 128], sc.t[:, kc, :], kc == 0, kc == KC - 1)
                      for kc in range(KC)], reads=[w, sc], writes=[p])
            for j in range(4):
                jj = b * 4 + j
                K.op(DVE, lambda e, j=j, jj=jj, p=p, l=l: e.tensor_scalar(
                    modT.t[:, l, jj, :], p.t[:, 2 * j:2 * j + 2], bmod_sb.t[:, l, jj:jj + 1], None, ALU.add),
                    reads=[p, bmod_sb], writes=[modT])
        for l in range(depth):
            for v in range(2):
                K.op(DVE, lambda e, l=l, v=v: e.scalar_tensor_tensor(
                    gs1.t[:, l, :, v], modT.t[:, l, 16:32, v], 1.0, n1g_sb.t[:, l, :], ALU.add, ALU.mult),
                    reads=[modT, n1g_sb], writes=[gs1])
                K.op(DVE, lambda e, l=l, v=v: e.scalar_tensor_tensor(
                    gs2.t[:, l, :, v], modT.t[:, l, 64:80, v], 1.0, n2g_sb.t[:, l, :], ALU.add, ALU.mult),
                    reads=[modT, n2g_sb], writes=[gs2])
        K.barrier()
        st.close()

    def layer_params(l):
        K.dma(SP, cdw_sb.t[:], cdw[l], writes=[cdw_sb])
        K.dma(SP, cdb_sb.t[:], cdb[l], writes=[cdb_sb])
        K.dma(SP, clg_sb.t[:], clg[l], writes=[clg_sb])
        K.dma(SP, clb_sb.t[:], clb[l], writes=[clb_sb])
        K.dma(SP, fdw_sb.t[:], fdw[l], writes=[fdw_sb])
        K.dma(SP, fdb_sb.t[:], fdb[l], writes=[fdb_sb])
        K.dma(SP, dec_sb.t[:, 0, :], decf[l].partition_broadcast(128), writes=[dec_sb])
        K.dma(SP, dec_sb.t[:, 1, :], decb[l].partition_broadcast(128), writes=[dec_sb])
        K.op(ACT, lambda e: e.activation(lg_sb.t[:], dec_sb.t[:], AF.Exp, scale=-1.0), reads=[dec_sb], writes=[lg_sb])
        K.op(DVE, lambda e: e.tensor_scalar_add(lg_sb.t[:], lg_sb.t[:], 1.0), reads=[lg_sb], writes=[lg_sb])
        K.op(ACT, lambda e: e.activation(lg_sb.t[:], lg_sb.t[:], AF.Ln), reads=[lg_sb], writes=[lg_sb])
        K.op(DVE, lambda e: e.tensor_scalar_mul(lg_sb.t[:], lg_sb.t[:], -1.0), reads=[lg_sb], writes=[lg_sb])
        isq = float(DH ** -0.5)
        st = ExitStack()
        K.new_phase()
        tmpA = K.sb(st, "tmpA", [128, 128], F32)
        tk = K.sb(st, "tk", [128, 2, NH], F32)
        for h in range(NH):
            lf = lg_sb.t[:, 0, h:h + 1]
            lb = lg_sb.t[:, 1, h:h + 1]
            K.op(ACT, lambda e, h=h, lf=lf: e.activation(MASK.t[:, h, :], cst["c_dpos"].t[:], AF.Exp, scale=lf),
                 reads=[cst["c_dpos"], lg_sb], writes=[MASK])
            K.op(DVE, lambda e, h=h: e.tensor_tensor(MASK.t[:, h, :], MASK.t[:, h, :], cst["c_ge"].t[:], ALU.mult),
                 reads=[MASK, cst["c_ge"]], writes=[MASK])
            K.op(ACT, lambda e, h=h, lb=lb: e.activation(tmpA.t[:], cst["c_dneg"].t[:], AF.Exp, scale=lb),
                 reads=[cst["c_dneg"], lg_sb], writes=[tmpA])
            K.op(DVE, lambda e, h=h: e.tensor_tensor(tmpA.t[:], tmpA.t[:], cst["c_le"].t[:], ALU.mult),
                 reads=[tmpA, cst["c_le"]], writes=[tmpA])
            K.op(DVE, lambda e, h=h: e.tensor_tensor(MASK.t[:, h, :], MASK.t[:, h, :], tmpA.t[:], ALU.add),
                 reads=[MASK, tmpA], writes=[MASK])
            K.op(DVE, lambda e, h=h: e.tensor_scalar_mul(MASK.t[:, h, :], MASK.t[:, h, :], isq),
                 reads=[MASK], writes=[MASK])
            K.op(ACT, lambda e, h=h, lf=lf: e.activation(QDF.t[:, h, :], cst["c_n1"].t[:], AF.Exp, scale=lf),
                 reads=[cst["c_n1"], lg_sb], writes=[QDF])
            K.op(ACT, lambda e, h=h, lb=lb: e.activation(QDB.t[:, h, :], cst["c_cn"].t[:], AF.Exp, scale=lb),
                 reads=[cst["c_cn"], lg_sb], writes=[QDB])
        K.op(DVE, lambda e: e.tensor_scalar_mul(tk.t[:, 0, :], lg_sb.t[:, 0, :], cst["c_kf"].t[:, 0:1]),
             reads=[lg_sb, cst["c_kf"]], writes=[tk])
        K.op(DVE, lambda e: e.tensor_scalar_mul(tk.t[:, 1, :], lg_sb.t[:, 1, :], cst["c_kb"].t[:, 0:1]),
             reads=[lg_sb, cst["c_kb"], tk], writes=[tk])
        K.op(ACT, lambda e: e.activation(KDF.t[:], tk.t[:, 0, :], AF.Exp), reads=[tk], writes=[KDF])
        K.op(ACT, lambda e: e.activation(KDB.t[:], tk.t[:, 1, :], AF.Exp), reads=[tk], writes=[KDB])
        K.op(DVE, lambda e: e.tensor_scalar_mul(KDF.t[:], KDF.t[:], isq), reads=[KDF], writes=[KDF])
        K.op(DVE, lambda e: e.tensor_scalar_mul(KDB.t[:], KDB.t[:], isq), reads=[KDB], writes=[KDB])
        K.op(ACT, lambda e: e.activation(CDF.t[:], lg_sb.t[:, 0, :], AF.Exp, scale=128.0), reads=[lg_sb], writes=[CDF])
        K.op(ACT, lambda e: e.activation(CDB.t[:], lg_sb.t[:, 1, :], AF.Exp, scale=128.0), reads=[lg_sb], writes=[CDB])
        K.barrier()
        st.close()

    def b3(T):
        return T.t[:, :].unsqueeze(2).broadcast_to([128, NH, 128])

    def v3(ap):
        return ap.rearrange("p (h n) -> p h n", h=NH)

    def kview(ap2d, t0, n):
        return ap2d.rearrange("(k p) t -> p k t", p=128)[:, :, t0:t0 + n]

    def norm_mod(st_bufs, xs, hT, TTs, gs, l, v, shift_lo):
        sq, pss, rt, rstd, tmp = st_bufs
        for g4 in range(4):
            sq_ = sq[g4 % 2]
            K.op(ACT, lambda e, g4=g4, sq_=sq_: e.activation(sq_.t[:, :, :TTs], xs.t[:, g4 * 4:(g4 + 1) * 4, :TTs], AF.Square),
                 reads=[xs], writes=[sq_])
            K.mm([(pss.t[:, :TTs], ones_bf.t[:], sq_.t[:, kk, :TTs], g4 == 0 and kk == 0, g4 == 3 and kk == 3) for kk in range(4)],
                 reads=[ones_bf, sq_], writes=[pss])
        K.op(ACT, lambda e: e.activation(rt.t[:, :TTs], pss.t[:, :TTs], AF.Sqrt, bias=epsb.t[:], scale=1.0 / D),
             reads=[pss, epsb], writes=[rt])
        K.op(DVE, lambda e: e.reciprocal(rstd.t[:, :TTs], rt.t[:, :TTs]), reads=[rt], writes=[rstd])
        for kc in range(KC):
            tb = tmp[kc % 2]
            K.op(DVE, lambda e, kc=kc, tb=tb: e.scalar_tensor_tensor(
                tb.t[:, :TTs], xs.t[:, kc, :TTs], gs.t[:, l, kc, v:v + 1], rstd.t[:, :TTs], ALU.mult, ALU.mult),
                reads=[xs, gs, rstd], writes=[tb])
            K.op(ACT, lambda e, kc=kc, tb=tb: e.activation(
                hT.t[:, kc, :TTs], tb.t[:, :TTs], AF.Identity, bias=modT.t[:, l, shift_lo + kc, v:v + 1], scale=1.0),
                reads=[tb, modT], writes=[hT])

    def phase_A(s, l):
        st = ExitStack()
        K.new_phase()
        par = l % 2
        TTs = s.TT
        xsrc = s.src if l == 0 else s.xs
        xsl = [K.sb(st, "A_x", [128, KC, TTs], F32, dma=True)]
        hTs = [K.sb(st, f"A_hT{i}", [128, KC, TTs], BF16) for i in range(1)]
        sq = [K.sb(st, f"A_sq{i}", [128, 4, TTs], BF16) for i in range(2)]
        rt = K.sb(st, "A_rt", [128, TTs], F32)
        rstd = K.sb(st, "A_rstd", [128, TTs], F32)
        tmp = [K.sb(st, f"A_tmp{i}", [128, TTs], F32) for i in range(2)]
        wsl = [K.sb(st, f"A_w{i}", [128, KC, 512], BF16, dma=True) for i in range(2)]
        qst = K.sb(st, "A_qst", [128, NH, TTs], BF16, dma=True)
        kst = K.sb(st, "A_kst", [128, NH, TTs], BF16, dma=True)
        qfst = K.sb(st, "A_qfst", [128, NH, TTs], BF16, dma=True)
        qbst = K.sb(st, "A_qbst", [128, NH, TTs], BF16, dma=True)
        sgst = K.sb(st, "A_sgst", [128, 4, TTs], BF16, dma=True)
        ust = K.sb(st, "A_ust", [128, 2, TTs], BF16, dma=True)
        ktst = K.sb(st, "A_ktst", [128, s.cpt, RW], BF16, dma=True)
        vtst = K.sb(st, "A_vtst", [128, s.cpt, RW], BF16, dma=True)
        a_sb = [K.sb(st, f"A_a{i}", [128, TTs], F32) for i in range(2)]
        sig = K.sb(st, "A_sig", [128, TTs], F32)
        qf32 = [K.sb(st, f"A_qf{i}", [128, TTs], F32) for i in range(2)]
        t1 = [K.sb(st, f"A_t1{i}", [128, TTs], F32) for i in range(1)] * 2
        t2 = [K.sb(st, f"A_t2{i}", [128, TTs], F32) for i in range(1)] * 2
        cosb = K.sb(st, "A_cos", [128, TTs], F32, dma=True)
        sinb = K.sb(st, "A_sin", [128, TTs], F32, dma=True)
        pss = K.ps(st, "A_pss", [128, 512], F32)
        pacc = [K.ps(st, f"A_pacc{i}", [128, 512], F32) for i in range(4)]
        prope = [K.ps(st, f"A_prope{i}", [128, 512], F32) for i in range(2)]
        ptr = K.ps(st, "A_ptr", [128, RW], BF16)
        acc_i = 0
        w_i = 0
        cvt_wait(l, "in")
        for ti in range(s.nt):
            t0 = ti * TTs
            xs = xsl[0]
            hT = hTs[0]
            if not s.is_ctx:
                cvt_pump(3)
            K.dma(SP, xs.t[:], kview(xsrc, t0, TTs), writes=[xs])
            if s.rope:
                K.dma(SP, cosb.t[:], ropec[:, t0:t0 + TTs], writes=[cosb])
                K.dma(SP, sinb.t[:], ropes[:, t0:t0 + TTs], writes=[sinb])
            norm_mod((sq, pss, rt, rstd, tmp), xs, hT, TTs, gs1, l, s.v, 0)
            rope_i = 0
            for b in range(12):
                w = wsl[w_i % 2]
                w_i += 1
                K.dma(SP, w.t[:].rearrange("p k n -> p (k n)"), wb_in[par, b], writes=[w])
                if b in (4, 5):
                    for j in range(s.cpt):
                        p = pacc[acc_i % 4]
                        acc_i += 1
                        K.mm([(p.t[:, :], hT.t[:, kc, j * 128:(j + 1) * 128], w.t[:, kc, :], kc == 0, kc == KC - 1)
                              for kc in range(KC)], reads=[hT, w], writes=[p])
                        half = b - 4
                        if j % 2 == 0:
                            K.op(ACT, lambda e, j=j, p=p, half=half: e.copy(vtst.t[:, j, half * 512:(half + 1) * 512], p.t[:, :]),
                                 reads=[p], writes=[vtst])
                        else:
                            K.op(DVE, lambda e, j=j, p=p, half=half: e.tensor_copy(vtst.t[:, j, half * 512:(half + 1) * 512], p.t[:, :]),
                                 reads=[p], writes=[vtst])
                    continue
                for j in range(4):
                    p = pacc[acc_i % 4]
                    acc_i += 1
                    K.mm([(p.t[:, :TTs], w.t[:, kc, j * 128:(j + 1) * 128], hT.t[:, kc, :TTs], kc == 0, kc == KC - 1)
                          for kc in range(KC)], reads=[hT, w], writes=[p])
                    if b < 4:
                        h = (b % 2) * 4 + j
                        dst = qst if b < 2 else kst
                        if not s.rope:
                            K.op(ACT, lambda e, p=p, dst=dst, h=h: e.copy(dst.t[:, h, :TTs], p.t[:, :TTs]),
                                 reads=[p], writes=[dst])
                        else:
                            r = rope_i % 2
                            rope_i += 1
                            qf = qf32[r]
                            pr = prope[r]
                            K.op(ACT, lambda e, p=p, qf=qf: e.copy(qf.t[:, :TTs], p.t[:, :TTs]), reads=[p], writes=[qf])
                            K.mm([(pr.t[:, :TTs], cst["c_perm"].t[:], qf.t[:, :TTs], True, True)],
                                 reads=[cst["c_perm"], qf], writes=[pr])
                            K.op(POOL, lambda e, qf=qf, r=r: e.tensor_tensor(t1[r].t[:, :TTs], qf.t[:, :TTs], cosb.t[:, :TTs], ALU.mult),
                                 reads=[qf, cosb], writes=[t1[r]])
                            K.op(DVE, lambda e, pr=pr, r=r: e.tensor_tensor(t2[r].t[:, :TTs], pr.t[:, :TTs], sinb.t[:, :TTs], ALU.mult),
                                 reads=[pr, sinb], writes=[t2[r]])
                            K.op(POOL, lambda e, r=r, dst=dst, h=h: e.tensor_tensor(dst.t[:, h, :TTs], t1[r].t[:, :TTs], t2[r].t[:, :TTs], ALU.add),
                                 reads=[t1[r], t2[r]], writes=[dst])
                    if b < 2:
                        cp_ = TTs // 128
                        for (dstq, TAB) in ((qfst, QDF), (qbst, QDB)):
                            K.op(DVE, lambda e, dstq=dstq, TAB=TAB, h=h: e.tensor_tensor(
                                dstq.t[:, h, :TTs].rearrange("p (c n) -> p c n", n=128),
                                qst.t[:, h, :TTs].rearrange("p (c n) -> p c n", n=128),
                                TAB.t[:, h, :].unsqueeze(1).broadcast_to([128, cp_, 128]), ALU.mult),
                                reads=[qst, TAB], writes=[dstq])
                    if b in (6, 7):
                        K.op(ACT, lambda e, p=p, j=j: e.activation(sgst.t[:, j, :TTs], p.t[:, :TTs], AF.Silu),
                             reads=[p], writes=[sgst])
                    elif b >= 8:
                        pp = b - 8
                        if j < 2:
                            K.op(DVE, lambda e, p=p, j=j: e.tensor_copy(a_sb[j].t[:, :TTs], p.t[:, :TTs]),
                                 reads=[p], writes=[a_sb[j]])
                        else:
                            c = 2 * pp + (j - 2)
                            K.op(ACT, lambda e, p=p: e.activation(sig.t[:, :TTs], p.t[:, :TTs], AF.Sigmoid),
                                 reads=[p], writes=[sig])
                            K.op(DVE, lambda e, j=j, c=c: e.tensor_tensor(ust.t[:, j - 2, :TTs], a_sb[j - 2].t[:, :TTs], sig.t[:, :TTs], ALU.mult),
                                 reads=[a_sb[j - 2], sig], writes=[ust])
                if b in (6, 7):
                    hb = (b - 6) * 4
                    K.dma(POOL, s.sgT[hb * 128:(hb + 4) * 128, t0:t0 + TTs].rearrange("(j p) t -> p j t", p=128), sgst.t[:], reads=[sgst])
                if b >= 8:
                    c0 = 2 * (b - 8)
                    K.dma(POOL, s.uT[c0 * 128:(c0 + 2) * 128, t0:t0 + TTs].rearrange("(j p) t -> p j t", p=128), ust.t[:], reads=[ust])
                if b == 3:
                    for j in range(s.cpt):
                        K.tr([(ptr.t[:, h * 128:(h + 1) * 128], kst.t[:, h, j * 128:(j + 1) * 128]) for h in range(NH)],
                             ident_bf, reads=[kst], writes=[ptr])
                        K.op(DVE, lambda e, j=j: e.tensor_copy(ktst.t[:, j, :], ptr.t[:, :]), reads=[ptr], writes=[ktst])
            K.dma(POOL, kview(s.qT, t0, TTs), qst.t[:], reads=[qst])
            K.dma(POOL, kview(s.kT, t0, TTs), kst.t[:], reads=[kst])
            K.dma(POOL, kview(s.qfT, t0, TTs), qfst.t[:], reads=[qfst])
            K.dma(POOL, kview(s.qbT, t0, TTs), qbst.t[:], reads=[qbst])
            K.dma(POOL, s.ktok[t0:t0 + TTs, :].rearrange("(j p) f -> p j f", p=128), ktst.t[:], reads=[ktst])
            K.dma(POOL, s.vtok[t0:t0 + TTs, :].rearrange("(j p) f -> p j f", p=128), vtst.t[:], reads=[vtst])
        K.barrier()
        st.close()

    def sweep(s, direction, init_fn, store, final_fn):
        st = ExitStack()
        K.new_phase()
        fwd = direction == "f"
        S = K.sb(st, "SW_S", [128, RW], F32)
        S16 = [K.sb(st, f"SW_S16_{i}", [128, RW], BF16, dma=True) for i in range(2)]
        kt = [K.sb(st, f"SW_kt{i}", [128, RW], BF16, dma=True) for i in range(3)]
        vt = [K.sb(st, f"SW_vt{i}", [128, RW], BF16, dma=True) for i in range(3)]
        kd = [K.sb(st, f"SW_kd{i}", [128, RW], BF16) for i in range(2)]
        tmpS = K.sb(st, "SW_tmp", [128, RW], F32)
        pA = [K.ps(st, f"SW_pA{i}", [128, RW], F32) for i in range(2)]
        KD = KDF if fwd else KDB
        CD = CDF if fwd else CDB
        SD = s.SfD if fwd else s.SbD
        init_fn(S, st)
        order = list(range(s.nch)) if fwd else list(range(s.nch - 1, -1, -1))

        def load(i):
            c = order[i]
            K.dma(SP, kt[i % 3].t[:], s.ktok[c * 128:(c + 1) * 128, :], writes=[kt[i % 3]])
            K.dma(SP, vt[i % 3].t[:], s.vtok[c * 128:(c + 1) * 128, :], writes=[vt[i % 3]])

        for i in range(min(2, len(order))):
            load(i)
        for i, c in enumerate(order):
            if i + 2 < len(order):
                load(i + 2)
            k_, v_ = kt[i % 3], vt[i % 3]
            kd_ = kd[i % 2]
            pA_ = pA[i % 2]
            if store:
                s16 = S16[i % 2]
                K.op(ACT, lambda e, s16=s16: e.copy(s16.t[:], S.t[:]), reads=[S], writes=[s16])
                K.dma(POOL, SD[c], s16.t[:], reads=[s16])
            K.op(DVE, lambda e, k_=k_, kd_=kd_: e.tensor_tensor(v3(kd_.t[:]), v3(k_.t[:]), b3(KD), ALU.mult),
                 reads=[k_, KD], writes=[kd_])
            K.mm([(pA_.t[:, h * 128:(h + 1) * 128], kd_.t[:, h * 128:(h + 1) * 128], v_.t[:, h * 128:(h + 1) * 128], True, True)
                  for h in range(NH)], reads=[kd_, v_], writes=[pA_])
            K.op(DVE, lambda e: e.tensor_tensor(v3(tmpS.t[:]), v3(S.t[:]), b3(CD), ALU.mult),
                 reads=[S, CD], writes=[tmpS])
            K.op(DVE, lambda e, pA_=pA_: e.tensor_tensor(S.t[:], tmpS.t[:], pA_.t[:], ALU.add), reads=[tmpS, pA_], writes=[S])
        final_fn(S, st)
        K.barrier()
        st.close()

    def phase_B1(s, l):
        if s.is_ctx:
            def zero(S, st):
                K.op(DVE, lambda e: e.memset(S.t[:], 0.0), writes=[S])
            sweep(s, "f", zero, True, lambda S, st: K.op(DVE, lambda e: e.tensor_copy(sfin.t[:], S.t[:]), reads=[S], writes=[sfin]))
            sweep(s, "b", zero, True, lambda S, st: K.op(DVE, lambda e: e.tensor_copy(sbin.t[:], S.t[:]), reads=[S], writes=[sbin]))
            return
        def mk_init1(src, fcol):
            def f(S, st):
                K.op(DVE, lambda e: e.tensor_scalar_mul(S.t[:], src.t[:], flags_sb.t[:, fcol:fcol + 1]), reads=[src, flags_sb], writes=[S])
            return f

        def mk_final1(idx):
            def f(S, st):
                K.dma(SP, stx[idx * 128:(idx + 1) * 128, :], S.t[:], reads=[S], sem=K.sem("d_xch"))
            return f

        sweep(s, "f", mk_init1(sfin, 0), False, mk_final1(0))
        sweep(s, "b", mk_init1(sbin, 1), False, mk_final1(1))
        collective(stx, stg)

        def mk_init2(src, fown, fpart, grow):
            def f(S, st):
                g = K.sb(st, "SW_g", [128, RW], F32, dma=True)
                K.dma(SP, g.t[:], stg[grow * 128:(grow + 1) * 128, :], writes=[g])
                K.op(DVE, lambda e: e.tensor_scalar_mul(S.t[:], g.t[:], flags_sb.t[:, fpart:fpart + 1]), reads=[g, flags_sb], writes=[S])
                K.op(DVE, lambda e: e.scalar_tensor_tensor(S.t[:], src.t[:], flags_sb.t[:, fown:fown + 1], S.t[:], ALU.mult, ALU.add),
                     reads=[src, flags_sb, S], writes=[S])
            return f

        nop = lambda S, st: None
        sweep(s, "f", mk_init2(sfin, 0, 1, 0), True, nop)
        sweep(s, "b", mk_init2(sbin, 1, 0, 3), True, nop)

    def phase_B2(s, l):
        st = ExitStack()
        K.new_phase()
        TTs = s.TT
        cpt = s.cpt
        qTt = [K.sb(st, f"B2_qT{i}", [128, NH, TTs], BF16, dma=True) for i in range(2)]
        kTt = [K.sb(st, f"B2_kT{i}", [128, NH, TTs], BF16, dma=True) for i in range(2)]
        qfTt = [K.sb(st, f"B2_qfT{i}", [128, NH, TTs], BF16, dma=True) for i in range(2)]
        qbTt = [K.sb(st, f"B2_qbT{i}", [128, NH, TTs], BF16, dma=True) for i in range(2)]
        sgt = [K.sb(st, f"B2_sg{i}", [128, NH, TTs], BF16, dma=True) for i in range(2)]
        vtt = [K.sb(st, f"B2_vt{i}", [128, cpt, RW], BF16, dma=True) for i in range(2)]
        sft = [K.sb(st, f"B2_sf{i}", [128, cpt, RW], BF16, dma=True) for i in range(2)]
        sbt = [K.sb(st, f"B2_sb{i}", [128, cpt, RW], BF16, dma=True) for i in range(2)]
        mixst = [K.sb(st, f"B2_mix{i}", [128, NH, TTs], BF16, dma=True) for i in range(2)]
        Pb = [K.sb(st, f"B2_P{i}", [128, NH, 128], BF16) for i in range(2)]
        sqob = [K.sb(st, f"B2_sqo{i}", [128, NH, 128], BF16) for i in range(2)]
        rt = K.sb(st, "B2_rt", [128, NH, 128], F32)
        rinv = K.sb(st, "B2_rinv", [128, NH, 128], F32)
        on = K.sb(st, "B2_on", [128, NH, 128], F32)
        pSC = K.ps(st, "B2_pSC", [128, RW], F32)
        pOb = [K.ps(st, f"B2_pO{i}", [128, RW], F32) for i in range(2)]
        pSS = K.ps(st, "B2_pSS", [128, RW], F32)

        def load(ti):
            t0 = ti * TTs
            i = ti % 2
            K.dma(SP, qTt[i].t[:], kview(s.qT, t0, TTs), writes=[qTt[i]])
            K.dma(SP, kTt[i].t[:], kview(s.kT, t0, TTs), writes=[kTt[i]])
            K.dma(SP, qfTt[i].t[:], kview(s.qfT, t0, TTs), writes=[qfTt[i]])
            K.dma(SP, qbTt[i].t[:], kview(s.qbT, t0, TTs), writes=[qbTt[i]])
            K.dma(SP, sgt[i].t[:], kview(s.sgT, t0, TTs), writes=[sgt[i]])
            K.dma(SP, vtt[i].t[:], s.vtok[t0:t0 + TTs, :].rearrange("(j p) f -> p j f", p=128), writes=[vtt[i]])
            K.dma(SP, sft[i].t[:], s.SfD[ti * cpt:(ti + 1) * cpt].rearrange("j p f -> p j f"), writes=[sft[i]])
            K.dma(SP, sbt[i].t[:], s.SbD[ti * cpt:(ti + 1) * cpt].rearrange("j p f -> p j f"), writes=[sbt[i]])

        load(0)
        chunks = [(ti, j) for ti in range(s.nt) for j in range(cpt)]

        def front(n):
            ti, j = chunks[n]
            i = ti % 2
            cs = slice(j * 128, (j + 1) * 128)
            P = Pb[n % 2]
            K.mm([(pSC.t[:, h * 128:(h + 1) * 128], kTt[i].t[:, h, cs], qTt[i].t[:, h, cs], True, True) for h in range(NH)],
                 reads=[kTt[i], qTt[i]], writes=[pSC])
            K.op(DVE, lambda e, P=P: e.tensor_tensor(P.t[:].rearrange("p h n -> p (h n)"), pSC.t[:],
                                                     MASK.t[:].rearrange("p h n -> p (h n)"), ALU.mult),
                 reads=[pSC, MASK], writes=[P])

        front(0)
        for n, (ti, j) in enumerate(chunks):
            if j == 0 and ti + 1 < s.nt:
                load(ti + 1)
            i = ti % 2
            sg_, vt_, sf_, sb_, mx = sgt[i], vtt[i], sft[i], sbt[i], mixst[i]
            qf_, qb_ = qfTt[i], qbTt[i]
            P, sqo, pO = Pb[n % 2], sqob[n % 2], pOb[n % 2]
            cs = slice(j * 128, (j + 1) * 128)
            mms = []
            for h in range(NH):
                hs = slice(h * 128, (h + 1) * 128)
                mms.append((pO.t[:, hs], vt_.t[:, j, hs], P.t[:, h, :], True, False))
                mms.append((pO.t[:, hs], sf_.t[:, j, hs], qf_.t[:, h, cs], False, False))
                mms.append((pO.t[:, hs], sb_.t[:, j, hs], qb_.t[:, h, cs], False, True))
            K.mm(mms, reads=[vt_, P, sf_, qf_, sb_, qb_], writes=[pO])
            if n + 1 < len(chunks):
                front(n + 1)
            K.op(ACT, lambda e, sqo=sqo, pO=pO: e.activation(sqo.t[:].rearrange("p h n -> p (h n)"), pO.t[:], AF.Square),
                 reads=[pO], writes=[sqo])
            K.mm([(pSS.t[:, h * 128:(h + 1) * 128], ones_bf.t[:], sqo.t[:, h, :], True, True) for h in range(NH)],
                 reads=[ones_bf, sqo], writes=[pSS])
            K.op(ACT, lambda e: e.activation(rt.t[:].rearrange("p h n -> p (h n)"), pSS.t[:], AF.Sqrt,
                                             bias=epsb.t[:], scale=1.0 / DH), reads=[pSS, epsb], writes=[rt])
            K.op(DVE, lambda e: e.reciprocal(rinv.t[:], rt.t[:]), reads=[rt], writes=[rinv])
            K.op(DVE, lambda e, pO=pO: e.tensor_tensor(on.t[:].rearrange("p h n -> p (h n)"), pO.t[:],
                                                       rinv.t[:].rearrange("p h n -> p (h n)"), ALU.mult),
                 reads=[pO, rinv], writes=[on])
            K.op(POOL, lambda e, mx=mx, sg_=sg_, cs=cs: e.tensor_tensor(mx.t[:, :, cs], on.t[:], sg_.t[:, :, cs], ALU.mult),
                 reads=[on, sg_], writes=[mx])
            if j == cpt - 1:
                K.dma(POOL, kview(s.mixT[0:RW, :], ti * TTs, TTs), mx.t[:], reads=[mx])
        K.barrier()
        st.close()

    def phase_C(s, l):
        st = ExitStack()
        K.new_phase()
        TTs = s.TT
        HL = CK // 2
        W = TTs + 2 * HL
        ut = [K.sb(st, f"C_u{i}", [128, 8, W], BF16, dma=True) for i in range(2)]
        acc2 = [[K.sb(st, f"C_acc{c}_{i}", [128, TTs], F32) for c in range(8)] for i in range(2)]
        sq = K.sb(st, "C_sq", [128, 8, TTs], F32)
        mean = K.sb(st, "C_mean", [128, TTs], F32)
        msq = K.sb(st, "C_msq", [128, TTs], F32)
        var = K.sb(st, "C_var", [128, TTs], F32)
        rt = K.sb(st, "C_rt", [128, TTs], F32)
        rstd = K.sb(st, "C_rstd", [128, TTs], F32)
        tm = [K.sb(st, f"C_tm{i}", [128, TTs], F32) for i in range(2)]
        ost = [K.sb(st, f"C_o{i}", [128, 8, TTs], BF16, dma=True) for i in range(2)]
        pS1 = K.ps(st, "C_pS1", [128, 512], F32)
        pS2 = K.ps(st, "C_pS2", [128, 512], F32)
        pcv = [K.ps(st, f"C_pc{i}", [128, 512], F32) for i in range(2)]
        pc_i = [0]
        dg = K.sb(st, "C_dg", [128, 8, CK, 128], BF16)
        for c in range(8):
            for k in range(CK):
                E = DVE if (c * CK + k) % 2 == 0 else POOL
                K.op(E, lambda e, c=c, k=k: e.tensor_scalar_mul(dg.t[:, c, k, :], ident_bf.t[:], cdw_sb.t[:, c, k:k + 1]),
                     reads=[ident_bf, cdw_sb], writes=[dg])
        uview = s.uT.rearrange("(k p) t -> p k t", p=128)
        if not s.is_ctx:
            xs_ = K.sem("d_xch")
            for which, c0 in ((0, 0), (1, s.L - HL)):
                SP.e.dma_start(out=uhx[which * CW:(which + 1) * CW, 0:HL], in_=s.uT[:, c0:c0 + HL]).then_inc(xs_.h, 16)
                xs_.count += 16
            collective(uhx, uhg)

        def load(ti):
            u = ut[ti % 2]
            t0 = ti * TTs
            lo = max(t0 - HL, 0)
            hi = min(t0 + TTs + HL, s.L)
            if s.is_ctx and (lo != t0 - HL or hi != t0 + TTs + HL):
                K.op(POOL, lambda e, u=u: e.memset(u.t[:], 0.0), writes=[u])
            K.dma(SP, u.t[:, :, lo - (t0 - HL):hi - (t0 - HL)], uview[:, :, lo:hi], writes=[u])
            if not s.is_ctx:
                if ti == 0:
                    K.dma(SP, u.t[:, :, 0:HL], uhg[1 * CW:2 * CW, 0:HL].rearrange("(k p) t -> p k t", p=128), writes=[u])
                    K.op(DVE, lambda e, u=u: e.tensor_scalar_mul(u.t[:, :, 0:HL], u.t[:, :, 0:HL], flags_sb.t[:, 1:2]),
                         reads=[u, flags_sb], writes=[u])
                if ti == s.nt - 1:
                    K.dma(SP, u.t[:, :, W - HL:W], uhg[2 * CW:3 * CW, 0:HL].rearrange("(k p) t -> p k t", p=128), writes=[u])
                    K.op(DVE, lambda e, u=u: e.tensor_scalar_mul(u.t[:, :, W - HL:W], u.t[:, :, W - HL:W], flags_sb.t[:, 0:1]),
                         reads=[u, flags_sb], writes=[u])

        load(0)
        for ti in range(s.nt):
            if ti + 1 < s.nt:
                load(ti + 1)
            u = ut[ti % 2]
            o = ost[ti % 2]
            acc = acc2[ti % 2]
            for c in range(8):
                pc = pcv[pc_i[0] % 2]
                pc_i[0] += 1
                K.mm([(pc.t[:, :TTs], dg.t[:, c, k, :], u.t[:, c, k:k + TTs], k == 0, k == CK - 1) for k in range(CK)],
                     reads=[dg, u], writes=[pc])
                K.op(ACT, lambda e, c=c, pc=pc: e.activation(acc[c].t[:], pc.t[:, :TTs], AF.Identity, bias=cdb_sb.t[:, c:c + 1], scale=1.0),
                     reads=[pc, cdb_sb], writes=[acc[c]])
            for c in range(8):
                K.op(ACT, lambda e, c=c: e.activation(sq.t[:, c, :], acc[c].t[:], AF.Square), reads=[acc[c]], writes=[sq])
            K.mm([(pS1.t[:, :TTs], ones_f.t[:], acc[c].t[:], c == 0, c == 7) for c in range(8)],
                 reads=[ones_f] + acc, writes=[pS1])
            K.mm([(pS2.t[:, :TTs], ones_f.t[:], sq.t[:, c, :], c == 0, c == 7) for c in range(8)],
                 reads=[ones_f, sq], writes=[pS2])
            K.op(ACT, lambda e: e.mul(mean.t[:], pS1.t[:, :TTs], 1.0 / CW), reads=[pS1], writes=[mean])
            K.op(DVE, lambda e: e.tensor_tensor(msq.t[:], mean.t[:], mean.t[:], ALU.mult), reads=[mean], writes=[msq])
            K.op(DVE, lambda e: e.scalar_tensor_tensor(var.t[:], pS2.t[:, :TTs], 1.0 / CW, msq.t[:], ALU.mult, ALU.subtract),
                 reads=[pS2, msq], writes=[var])
            K.op(DVE, lambda e: e.tensor_scalar_max(var.t[:], var.t[:], 0.0), reads=[var], writes=[var])
            K.op(ACT, lambda e: e.activation(rt.t[:], var.t[:], AF.Sqrt, bias=epsb.t[:], scale=1.0), reads=[var, epsb], writes=[rt])
            K.op(DVE, lambda e: e.reciprocal(rstd.t[:], rt.t[:]), reads=[rt], writes=[rstd])
            for c in range(8):
                t_ = tm[c % 2]
                K.op(DVE, lambda e, c=c, t_=t_: e.tensor_tensor(t_.t[:], acc[c].t[:], mean.t[:], ALU.subtract),
                     reads=[acc[c], mean], writes=[t_])
                K.op(DVE, lambda e, t_=t_: e.tensor_tensor(t_.t[:], t_.t[:], rstd.t[:], ALU.mult), reads=[t_, rstd], writes=[t_])
                K.op(ACT, lambda e, c=c, t_=t_, o=o: e.activation(o.t[:, c, :], t_.t[:], AF.Silu, bias=clb_sb.t[:, c:c + 1],
                                                                  scale=clg_sb.t[:, c:c + 1]),
                     reads=[t_, clb_sb, clg_sb], writes=[o])
            K.dma(POOL, kview(s.mixT[RW:D, :], ti * TTs, TTs), o.t[:], reads=[o])
        K.barrier()
        st.close()

    def phase_D(s, l):
        st = ExitStack()
        K.new_phase()
        par = l % 2
        TTs = s.TT
        xsrc = s.src if l == 0 else s.xs
        xs = K.sb(st, "D_x", [128, KC, TTs], F32, dma=True)
        mx = [K.sb(st, f"D_mix{i}", [128, KC, TTs], BF16, dma=True) for i in range(2)]
        hT = K.sb(st, "D_hT", [128, KC, TTs], BF16)
        sq = [K.sb(st, f"D_sq{i}", [128, 4, TTs], BF16) for i in range(2)]
        rt = K.sb(st, "D_rt", [128, TTs], F32)
        rstd = K.sb(st, "D_rstd", [128, TTs], F32)
        tmp = [K.sb(st, f"D_tmp{i}", [128, TTs], F32) for i in range(2)]
        wsl = [K.sb(st, f"D_w{i}", [128, KC, 512], BF16, dma=True) for i in range(3)]
        gst = [K.sb(st, f"D_gst{i}", [128, 4, TTs], BF16, dma=True) for i in range(2)]
        pss = K.ps(st, "D_pss", [128, 512], F32)
        pacc = [K.ps(st, f"D_pacc{i}", [128, 512], F32) for i in range(6)]
        acc_i = 0
        w_i = 0
        g_i = 0
        mview = s.mixT.rearrange("(k p) t -> p k t", p=128)
        K.dma(SP, mx[0].t[:], mview[:, :, 0:TTs], writes=[mx[0]])
        cvt_wait(l, "out")
        cvt_wait(l, "up")
        for ti in range(s.nt):
            t0 = ti * TTs
            m_ = mx[ti % 2]
            if not s.is_ctx:
                cvt_pump(3)
            K.dma(SP, xs.t[:], kview(xsrc, t0, TTs), writes=[xs])
            if ti + 1 < s.nt:
                K.dma(SP, mx[(ti + 1) % 2].t[:], mview[:, :, t0 + TTs:t0 + 2 * TTs], writes=[mx[(ti + 1) % 2]])
            for b in range(4):
                w = wsl[w_i % 3]
                w_i += 1
                K.dma(SP, w.t[:].rearrange("p k n -> p (k n)"), wb_out[par, b], writes=[w])
                for j in range(4):
                    oc = b * 4 + j
                    p = pacc[acc_i % 6]
                    acc_i += 1
                    K.mm([(p.t[:, :TTs], w.t[:, kc, j * 128:(j + 1) * 128], m_.t[:, kc, :], kc == 0, kc == KC - 1)
                          for kc in range(KC)], reads=[m_, w], writes=[p])
                    K.op(DVE, lambda e, p=p, oc=oc: e.scalar_tensor_tensor(
                        xs.t[:, oc, :], p.t[:, :TTs], modT.t[:, l, 32 + oc, s.v:s.v + 1], xs.t[:, oc, :], ALU.mult, ALU.add),
                        reads=[p, modT, xs], writes=[xs])
            K.dma(POOL, kview(s.xs, t0, TTs), xs.t[:], reads=[xs])
            norm_mod((sq, pss, rt, rstd, tmp), xs, hT, TTs, gs2, l, s.v, 48)
            for b in range(22):
                w = wsl[w_i % 3]
                w_i += 1
                K.dma(SP, w.t[:].rearrange("p k n -> p (k n)"), wb_up[par, b], writes=[w])
                g = gst[g_i % 2]
                g_i += 1
                for j in range(4):
                    p = pacc[acc_i % 6]
                    acc_i += 1
                    K.mm([(p.t[:, :TTs], w.t[:, kc, j * 128:(j + 1) * 128], hT.t[:, kc, :], kc == 0, kc == KC - 1)
                          for kc in range(KC)], reads=[hT, w], writes=[p])
                    if j % 2 == 0:
                        K.op(ACT, lambda e, p=p, g=g, j=j: e.copy(g.t[:, j, :], p.t[:, :TTs]), reads=[p], writes=[g])
                    else:
                        K.op(DVE, lambda e, p=p, g=g, j=j: e.tensor_copy(g.t[:, j, :], p.t[:, :TTs]), reads=[p], writes=[g])
                dst = s.gateT if b < 11 else s.valT
                bb = b % 11
                K.dma(POOL, dst[bb * 512:(bb + 1) * 512, t0:t0 + TTs].rearrange("(j p) t -> p j t", p=128), g.t[:], reads=[g])
        K.barrier()
        st.close()

    def phase_E1(s, l):
        st = ExitStack()
        K.new_phase()
        TTs = s.TT
        HL = GW
        W = TTs + 2 * HL
        dgs = [K.sb(st, f"E1_dg{i}", [128, 4, 9, 128], BF16) for i in range(2)]
        gsl = [K.sb(st, f"E1_g{i}", [128, 4, W], BF16, dma=True) for i in range(3)]
        vsl = [K.sb(st, f"E1_v{i}", [128, 4, TTs], BF16, dma=True) for i in range(3)]
        gls = [K.sb(st, f"E1_gl{i}", [128, 4, W + 2], BF16) for i in range(2)]
        grs = [K.sb(st, f"E1_gr{i}", [128, 4, W + 2], BF16) for i in range(2)]
        sls = [K.sb(st, f"E1_sl{i}", [128, TTs], F32) for i in range(2)]
        asts = [K.sb(st, f"E1_a{i}", [128, 4, TTs], BF16, dma=True) for i in range(2)]
        pcv = [K.ps(st, f"E1_pc{i}", [128, 512], F32) for i in range(4)]
        maskL = K.sb(st, "E1_mL", [128, W], BF16)
        maskR = K.sb(st, "E1_mR", [128, W], BF16)
        gview = s.gateT.rearrange("(k p) t -> p k t", p=128)
        vview = s.valT.rearrange("(k p) t -> p k t", p=128)
        if not s.is_ctx:
            xs_ = K.sem("d_xch")
            for which, c0 in ((0, 0), (1, s.L - HL)):
                for q4 in range(4):
                    r0, r1 = q4 * (DFF // 4), (q4 + 1) * (DFF // 4)
                    SP.e.dma_start(out=ghx[which * DFF + r0:which * DFF + r1, :], in_=s.gateT[r0:r1, c0:c0 + HL]).then_inc(xs_.h, 16)
                    xs_.count += 16
            collective(ghx, ghg)
            for b_ in gls + grs:
                K.op(POOL, lambda e, b_=b_: e.memset(b_.t[:], 0.0), writes=[b_])
            K.op(DVE, lambda e: e.memset(maskL.t[:], 1.0), writes=[maskL])
            K.op(DVE, lambda e: e.memset(maskR.t[:], 1.0), writes=[maskR])
            K.op(DVE, lambda e: e.memset(maskL.t[:].rearrange("p (r w) -> p r w", w=GW)[:, :, GW - 1:GW], 0.0), writes=[maskL])
            K.op(DVE, lambda e: e.memset(maskR.t[:].rearrange("p (r w) -> p r w", w=GW)[:, :, 0:1], 0.0), writes=[maskR])
        if s.is_ctx:
            taps = [(3, -1, 0), (4, 0, 0), (5, 1, 0)]
        else:
            taps = [((dr + 1) * 3 + (dc + 1), dr * GW + dc, dc) for dr in (-1, 0, 1) for dc in (-1, 0, 1)]
        items = [(g, ti) for g in range(11) for ti in range(s.nt)]

        def load(n):
            g, ti = items[n]
            t0 = ti * TTs
            gb = gsl[n % 3]
            vb = vsl[n % 3]
            lo = max(t0 - HL, 0)
            hi = min(t0 + TTs + HL, s.L)
            if s.is_ctx and (lo != t0 - HL or hi != t0 + TTs + HL):
                K.op(POOL, lambda e, gb=gb: e.memset(gb.t[:], 0.0), writes=[gb])
            K.dma(SP, gb.t[:, :, lo - (t0 - HL):hi - (t0 - HL)], gview[:, g * 4:(g + 1) * 4, lo:hi], writes=[gb])
            if not s.is_ctx:
                if ti == 0:
                    r0 = 1 * DFF + g * 512
                    K.dma(SP, gb.t[:, :, 0:HL], ghg[r0:r0 + 512, :].rearrange("(k p) t -> p k t", p=128), writes=[gb])
                    K.op(DVE, lambda e, gb=gb: e.tensor_scalar_mul(gb.t[:, :, 0:HL], gb.t[:, :, 0:HL], flags_sb.t[:, 1:2]),
                         reads=[gb, flags_sb], writes=[gb])
                if ti == s.nt - 1:
                    r0 = 2 * DFF + g * 512
                    K.dma(SP, gb.t[:, :, W - HL:W], ghg[r0:r0 + 512, :].rearrange("(k p) t -> p k t", p=128), writes=[gb])
                    K.op(DVE, lambda e, gb=gb: e.tensor_scalar_mul(gb.t[:, :, W - HL:W], gb.t[:, :, W - HL:W], flags_sb.t[:, 0:1]),
                         reads=[gb, flags_sb], writes=[gb])
            K.dma(SP, vb.t[:], vview[:, g * 4:(g + 1) * 4, t0:t0 + TTs], writes=[vb])

        def build_dg(g):
            dg = dgs[g % 2]
            for jj in range(4):
                fc = g * 4 + jj
                for (wi, off, sel) in taps:
                    E = DVE if (jj + wi) % 2 == 0 else POOL
                    K.op(E, lambda e, jj=jj, wi=wi, fc=fc, dg=dg: e.tensor_scalar_mul(dg.t[:, jj, wi, :], ident_bf.t[:], fdw_sb.t[:, fc, wi:wi + 1]),
                         reads=[ident_bf, fdw_sb], writes=[dg])

        def prep(n):
            if s.is_ctx:
                return
            gb = gsl[n % 3]
            gl = gls[n % 2]
            gr = grs[n % 2]
            K.op(ACT, lambda e, gl=gl, gb=gb: e.copy(gl.t[:, :, 1:1 + W], gb.t[:]), reads=[gb], writes=[gl])
            K.op(DVE, lambda e, gl=gl: e.memset(gl.t[:, :, 1:1 + W].rearrange("p j (r w) -> p j r w", w=GW)[:, :, :, GW - 1:GW], 0.0),
                 reads=[gl], writes=[gl])
            K.op(ACT, lambda e, gr=gr, gb=gb: e.copy(gr.t[:, :, 1:1 + W], gb.t[:]), reads=[gb], writes=[gr])
            K.op(DVE, lambda e, gr=gr: e.memset(gr.t[:, :, 1:1 + W].rearrange("p j (r w) -> p j r w", w=GW)[:, :, :, 0:1], 0.0),
                 reads=[gr], writes=[gr])

        for n in range(min(2, len(items))):
            load(n)
        build_dg(0)
        prep(0)
        pc_i = 0
        for n, (g, ti) in enumerate(items):
            if n + 2 < len(items):
                load(n + 2)
            if n + 1 < len(items):
                prep(n + 1)
            t0 = ti * TTs
            dg = dgs[g % 2]
            gb = gsl[n % 3]
            vb = vsl[n % 3]
            gl = gls[n % 2]
            gr = grs[n % 2]
            ast = asts[n % 2]
            for jj in range(4):
                fc = g * 4 + jj
                pc = pcv[pc_i % 4]
                sl = sls[pc_i % 2]
                pc_i += 1
                mms = []
                for ix, (wi, off, sel) in enumerate(taps):
                    if sel == 0 or s.is_ctx:
                        rhs = gb.t[:, jj, HL + off:HL + off + TTs]
                    else:
                        src = gl if sel < 0 else gr
                        rhs = src.t[:, jj, 1 + HL + off:1 + HL + off + TTs]
                    mms.append((pc.t[:, :TTs], dg.t[:, jj, wi, :], rhs, ix == 0, ix == len(taps) - 1))
                K.mm(mms, reads=[dg, gb] + ([] if s.is_ctx else [gl, gr]), writes=[pc])
                K.op(ACT, lambda e, pc=pc, sl=sl, fc=fc: e.activation(sl.t[:], pc.t[:, :TTs], AF.Silu, bias=fdb_sb.t[:, fc:fc + 1], scale=1.0),
                     reads=[pc, fdb_sb], writes=[sl])
                K.op(DVE, lambda e, sl=sl, vb=vb, jj=jj, ast=ast: e.tensor_tensor(ast.t[:, jj, :], sl.t[:], vb.t[:, jj, :], ALU.mult),
                     reads=[sl, vb], writes=[ast])
            K.dma(POOL, s.actT[g * 512:(g + 1) * 512, t0:t0 + TTs].rearrange("(j p) t -> p j t", p=128), ast.t[:], reads=[ast])
            if ti == 0 and g + 1 < 11:
                build_dg(g + 1)
        K.barrier()
        st.close()

    def phase_E2(s, l):
        st = ExitStack()
        K.new_phase()
        par = l % 2
        TTs = s.TT
        xo = [K.sb(st, f"E_xo{i}", [128, TTs], F32, dma=True) for i in range(3)]
        act = [K.sb(st, f"E_act{i}", [128, FC, TTs], BF16, dma=True) for i in range(2)]
        wsl = [K.sb(st, f"E_w{i}", [128, FC, 128], BF16, dma=True) for i in range(3)]
        pacc = [K.ps(st, f"E_pacc{i}", [128, 512], F32) for i in range(4)]
        aview = s.actT.rearrange("(k p) t -> p k t", p=128)
        cvt_wait(l, "dn")
        K.dma(SP, act[0].t[:], aview[:, :, 0:TTs], writes=[act[0]])
        acc_i = 0
        w_i = 0
        for ti in range(s.nt):
            t0 = ti * TTs
            a_ = act[ti % 2]
            if ti + 1 < s.nt:
                K.dma(SP, act[(ti + 1) % 2].t[:], aview[:, :, t0 + TTs:t0 + 2 * TTs], writes=[act[(ti + 1) % 2]])
            if not s.is_ctx:
                cvt_pump(1)
            for oc in range(KC):
                w = wsl[w_i % 3]
                x_ = xo[w_i % 3]
                w_i += 1
                K.dma(SP, w.t[:].rearrange("p k n -> p (k n)"), wb_dn[par, oc], writes=[w])
                K.dma(SP, x_.t[:], s.xs[oc * 128:(oc + 1) * 128, t0:t0 + TTs], writes=[x_])
                p = pacc[acc_i % 4]
                acc_i += 1
                K.mm([(p.t[:, :TTs], w.t[:, fc, :], a_.t[:, fc, :], fc == 0, fc == FC - 1)
                      for fc in range(FC)], reads=[a_, w], writes=[p])
                K.op(DVE, lambda e, p=p, oc=oc, x_=x_: e.scalar_tensor_tensor(
                    x_.t[:], p.t[:, :TTs], modT.t[:, l, 80 + oc, s.v:s.v + 1], x_.t[:], ALU.mult, ALU.add),
                    reads=[p, modT, x_], writes=[x_])
                K.dma(POOL, s.xs[oc * 128:(oc + 1) * 128, t0:t0 + TTs], x_.t[:], reads=[x_])
        K.barrier()
        st.close()

    def phase_final(s):
        st = ExitStack()
        K.new_phase()
        TTs = s.TT
        xsl = [K.sb(st, f"F_x{i}", [128, KC, TTs], F32, dma=True) for i in range(2)]
        osl = [K.sb(st, f"F_o{i}", [128, KC, TTs], F32, dma=True) for i in range(2)]
        sq = K.sb(st, "F_sq", [128, KC, TTs], BF16)
        rt = K.sb(st, "F_rt", [128, TTs], F32)
        rstd = K.sb(st, "F_rstd", [128, TTs], F32)
        pss = K.ps(st, "F_pss", [128, 512], F32)
        K.dma(SP, xsl[0].t[:], kview(s.xs, 0, TTs), writes=[xsl[0]])
        for ti in range(s.nt):
            t0 = ti * TTs
            xs = xsl[ti % 2]
            o = osl[ti % 2]
            if ti + 1 < s.nt:
                K.dma(SP, xsl[(ti + 1) % 2].t[:], kview(s.xs, t0 + TTs, TTs), writes=[xsl[(ti + 1) % 2]])
            for g4 in range(4):
                K.op(ACT, lambda e, g4=g4, xs=xs: e.activation(sq.t[:, g4 * 4:(g4 + 1) * 4, :], xs.t[:, g4 * 4:(g4 + 1) * 4, :], AF.Square),
                     reads=[xs], writes=[sq])
            K.mm([(pss.t[:, :TTs], ones_bf.t[:], sq.t[:, kc, :], kc == 0, kc == KC - 1) for kc in range(KC)],
                 reads=[ones_bf, sq], writes=[pss])
            K.op(ACT, lambda e: e.activation(rt.t[:], pss.t[:, :TTs], AF.Sqrt, bias=epsb.t[:], scale=1.0 / D),
                 reads=[pss, epsb], writes=[rt])
            K.op(DVE, lambda e: e.reciprocal(rstd.t[:], rt.t[:]), reads=[rt], writes=[rstd])
            for kc in range(KC):
                E = DVE
                K.op(E, lambda e, kc=kc, xs=xs, o=o: e.scalar_tensor_tensor(
                    o.t[:, kc, :], xs.t[:, kc, :], fng_sb.t[:, kc:kc + 1], rstd.t[:], ALU.mult, ALU.mult),
                    reads=[xs, fng_sb, rstd], writes=[o])
            K.dma(SP, kview(outT, t0, TTs), o.t[:], reads=[o])
        K.barrier()
        st.close()

    convert_weights(0)
    cvt_pump(12)
    mod_phase()
    for l in range(depth):
        last = l == depth - 1
        if l + 1 < depth:
            convert_weights(l + 1)
        layer_params(l)
        K.barrier()
        for s in (sctx, slat):
            phase_A(s, l)
            phase_B1(s, l)
            phase_B2(s, l)
            if s.is_ctx and last:
                continue
            phase_C(s, l)
            phase_D(s, l)
            phase_E1(s, l)
            phase_E2(s, l)
    phase_final(slat)
    K.stack.close()
    return nc


def _fm(vec, nk):
    return np.ascontiguousarray(np.asarray(vec, np.float32).reshape(nk, 128).T)


def _blk(w, ncols_blk):
    Kd, N = w.shape
    kc = Kd // 128
    nb = N // ncols_blk
    a = w.reshape(kc, 128, nb, ncols_blk).transpose(2, 1, 0, 3)
    return np.ascontiguousarray(a).reshape(nb, 128, kc * ncols_blk)


def prepare_shared(inputs, depth, L):
    sh = {}
    w_mod = np.asarray(inputs["w_mod"], np.float32)
    sh["w_mod"] = np.stack([_blk(w_mod[l], 512).reshape(24, 128, KC, 512) for l in range(depth)])
    sh["b_mod"] = np.stack([_fm(inputs["b_mod"][l], 96) for l in range(depth)])
    sh["norm1_g"] = np.stack([_fm(inputs["norm1_g"][l], KC) for l in range(depth)])
    sh["norm2_g"] = np.stack([_fm(inputs["norm2_g"][l], KC) for l in range(depth)])
    w_in = np.asarray(inputs["w_in"], np.float32)
    wl = []
    for l in range(depth):
        w = w_in[l]
        cols = [w[:, b * 512:(b + 1) * 512] for b in range(8)]
        for pp in range(4):
            a = w[:, 4096 + pp * 256:4096 + (pp + 1) * 256]
            bg = w[:, 5120 + pp * 256:5120 + (pp + 1) * 256]
            cols.append(np.concatenate([a, bg], axis=1))
        wl.append(_blk(np.concatenate(cols, axis=1), 512))
    sh["w_in"] = np.stack(wl)
    sh["w_out"] = np.stack([_blk(np.asarray(inputs["w_out"][l], np.float32), 512) for l in range(depth)])
    sh["w_up"] = np.stack([_blk(np.asarray(inputs["ffn_w_up"][l], np.float32), 512) for l in range(depth)])
    sh["w_down"] = np.stack([_blk(np.asarray(inputs["ffn_w_down"][l], np.float32), 128) for l in range(depth)])
    sh["ret_decay_f"] = np.ascontiguousarray(np.asarray(inputs["ret_decay_f"], np.float32)[:depth])
    sh["ret_decay_b"] = np.ascontiguousarray(np.asarray(inputs["ret_decay_b"], np.float32)[:depth])
    cw = np.asarray(inputs["conv_dw_w"], np.float32)
    sh["conv_dw_w"] = np.stack([np.ascontiguousarray(cw[l].T.reshape(8, 128, CK).transpose(1, 0, 2)) for l in range(depth)])
    sh["conv_dw_b"] = np.stack([_fm(inputs["conv_dw_b"][l], 8) for l in range(depth)])
    sh["conv_ln_g"] = np.stack([_fm(inputs["conv_ln_g"][l], 8) for l in range(depth)])
    sh["conv_ln_b"] = np.stack([_fm(inputs["conv_ln_b"][l], 8) for l in range(depth)])
    fw = np.asarray(inputs["ffn_dw_w"], np.float32)
    sh["ffn_dw_w"] = np.stack([np.ascontiguousarray(fw[l].reshape(9, FC, 128).transpose(2, 1, 0)) for l in range(depth)])
    sh["ffn_dw_b"] = np.stack([_fm(inputs["ffn_dw_b"][l], FC) for l in range(depth)])
    sh["final_norm_g"] = _fm(inputs["final_norm_g"], KC)
    rc, rs = _rope_tables(L)
    sh["rope_c"] = rc
    sh["rope_s"] = rs
    sh.update(_consts())
    return sh


def run(inputs, depth, ncores, trace=False):
    x = np.asarray(inputs["x"], np.float32)
    B, L, _ = x.shape
    assert 2 * B == ncores
    Lc = L // 2
    nc = build_program(Lc, depth, ncores)
    sh = prepare_shared(inputs, depth, L)
    rc, rs = sh.pop("rope_c"), sh.pop("rope_s")
    c = np.asarray(inputs["c"], np.float32)
    cc = np.asarray(inputs["c_ctx"], np.float32)
    ctx = np.asarray(inputs["ctx"], np.float32)
    in_maps = []
    for r in range(ncores):
        b, half = r // 2, r % 2
        m = dict(sh)
        sl = slice(half * Lc, (half + 1) * Lc)
        m["xT"] = np.ascontiguousarray(x[b, sl].T)
        m["ctxT"] = np.ascontiguousarray(ctx[b].T)
        m["cvec"] = np.ascontiguousarray(np.stack([_fm(c[b], KC), _fm(cc, KC)], axis=-1))
        m["rope_c"] = np.ascontiguousarray(rc[:, sl])
        m["rope_s"] = np.ascontiguousarray(rs[:, sl])
        fl = np.zeros((128, 2), np.float32)
        fl[:, half] = 1.0
        m["flags"] = fl
        in_maps.append(m)
    res = run_bass_kernel_spmd(nc, in_maps, core_ids=list(range(ncores)), **({"trace": True} if trace else {}))
    out = np.empty((B, L, D), np.float32)
    for r in range(ncores):
        b, half = r // 2, r % 2
        out[b, half * Lc:(half + 1) * Lc] = res.results[r]["outT"].T
    return out, res


def kernel(**inputs):
    out, _ = run(inputs, 4, 8)
    return out
```
